# Optimizing a Trainium2 kernel written in Bass

```python
import math
import jax, jax.numpy as jnp
from jax import lax
import numpy as np

D_MODEL = 1024
BATCH = 4
SEQ = 4096
DEPTH = 2

N_MIXERS = 2
N_MLA_LAYERS = (DEPTH + 1) // 2
N_DIFF_LAYERS = DEPTH // 2

MLA_NOPE = 128
MLA_ROPE = 64
MLA_V = 128
MLA_HEADS = D_MODEL // MLA_V
MLA_Q_LORA = 384
MLA_KV_LORA = 256
MLA_ROPE_THETA = 10000.0

DIFF_HEAD_DIM = 64
DIFF_HEADS = D_MODEL // (2 * DIFF_HEAD_DIM)
DIFF_ROT_DIM = DIFF_HEAD_DIM // 4
ROPE_THETA = 500000.0

D_FF = 4 * D_MODEL
Q_BLOCK = 128
DEEPNORM_ALPHA = (2 * DEPTH) ** 0.25
DEEPNORM_BETA = (8 * DEPTH) ** -0.25
LN_EPS = 1e-5
RMS_EPS = 1e-6

kernel_name = "hybrid_mla_diffattn_deepnorm"


def _layernorm(x, g, b):
    xf = x.astype(jnp.float32)
    mu = jnp.mean(xf, axis=-1, keepdims=True)
    var = jnp.mean(jnp.square(xf - mu), axis=-1, keepdims=True)
    y = (xf - mu) * lax.rsqrt(var + LN_EPS) * g.astype(jnp.float32) + b.astype(jnp.float32)
    return y.astype(x.dtype)


def _rmsnorm(x, g):
    xf = x.astype(jnp.float32)
    y = xf * lax.rsqrt(jnp.mean(jnp.square(xf), axis=-1, keepdims=True) + RMS_EPS)
    return (y * g.astype(jnp.float32)).astype(x.dtype)


def _rope(x, positions, theta):
    d = x.shape[-1]
    inv_freq = theta ** (-jnp.arange(0, d, 2, dtype=jnp.float32) / d)
    ang = positions.astype(jnp.float32)[:, :, None] * inv_freq
    cos = jnp.cos(ang)[:, :, None, :]
    sin = jnp.sin(ang)[:, :, None, :]
    xf = x.astype(jnp.float32)
    x1, x2 = xf[..., : d // 2], xf[..., d // 2:]
    out = jnp.concatenate([x1 * cos - x2 * sin, x2 * cos + x1 * sin], axis=-1)
    return out.astype(x.dtype)


def _partial_rope(x, positions, rot_dim, theta):
    return jnp.concatenate([_rope(x[..., :rot_dim], positions, theta), x[..., rot_dim:]], axis=-1)


def _to_blocks(t):
    b, s = t.shape[0], t.shape[1]
    return t.reshape(b, s // Q_BLOCK, Q_BLOCK, *t.shape[2:]).swapaxes(0, 1)


def _from_blocks(t):
    n, b, qb = t.shape[0], t.shape[1], t.shape[2]
    return t.swapaxes(0, 1).reshape(b, n * qb, *t.shape[3:])


def _causal_probs(q_blk, k, q_start, scale):
    s = jnp.einsum('bqhd,bkhd->bhqk', q_blk, k).astype(jnp.float32) * scale
    q_idx = q_start + jnp.arange(q_blk.shape[1])
    k_idx = jnp.arange(k.shape[1])
    s = jnp.where(k_idx[None, :] <= q_idx[:, None], s, -jnp.inf)
    return jax.nn.softmax(s, axis=-1)


def _mla(x, positions, w_dq, q_norm, w_uq, w_dkv, kv_norm, w_ukv, w_o):
    b, s, _ = x.shape
    h = MLA_HEADS
    cq = _rmsnorm(x @ w_dq, q_norm)
    q = (cq @ w_uq).reshape(b, s, h, MLA_NOPE + MLA_ROPE)
    q = jnp.concatenate([q[..., :MLA_NOPE], _rope(q[..., MLA_NOPE:], positions, MLA_ROPE_THETA)], axis=-1)
    ckv = x @ w_dkv
    c_kv = _rmsnorm(ckv[..., :MLA_KV_LORA], kv_norm)
    k_pe = _rope(ckv[..., None, MLA_KV_LORA:], positions, MLA_ROPE_THETA)
    kv = (c_kv @ w_ukv).reshape(b, s, h, MLA_NOPE + MLA_V)
    k = jnp.concatenate([kv[..., :MLA_NOPE], jnp.broadcast_to(k_pe, (b, s, h, MLA_ROPE))], axis=-1)
    v = kv[..., MLA_NOPE:]
    scale = (MLA_NOPE + MLA_ROPE) ** -0.5
    starts = jnp.arange(s // Q_BLOCK) * Q_BLOCK

    def block(args):
        q_blk, start = args
        p = _causal_probs(q_blk, k, start, scale)
        return jnp.einsum('bhqk,bkhd->bqhd', p.astype(v.dtype), v)

    o = _from_blocks(lax.map(block, (_to_blocks(q), starts)))
    return o.reshape(b, s, h * MLA_V) @ w_o


def _diff_attn(x, positions, w_qkv, lq1, lk1, lq2, lk2, subln, w_o, lambda_init):
    b, s, _ = x.shape
    h, dh = DIFF_HEADS, DIFF_HEAD_DIM
    q, k, v = jnp.split(x @ w_qkv, 3, axis=-1)
    q = _partial_rope(q.reshape(b, s, 2 * h, dh), positions, DIFF_ROT_DIM, ROPE_THETA).reshape(b, s, h, 2, dh)
    k = _partial_rope(k.reshape(b, s, 2 * h, dh), positions, DIFF_ROT_DIM, ROPE_THETA).reshape(b, s, h, 2, dh)
    v = v.reshape(b, s, h, 2 * dh)
    k1, k2 = k[..., 0, :], k[..., 1, :]
    lam = (jnp.exp(jnp.sum(lq1.astype(jnp.float32) * lk1.astype(jnp.float32)))
           - jnp.exp(jnp.sum(lq2.astype(jnp.float32) * lk2.astype(jnp.float32)))
           + lambda_init)
    scale = dh ** -0.5
    starts = jnp.arange(s // Q_BLOCK) * Q_BLOCK

    def block(args):
        q_blk, start = args
        p1 = _causal_probs(q_blk[..., 0, :], k1, start, scale)
        p2 = _causal_probs(q_blk[..., 1, :], k2, start, scale)
        a = p1 - lam * p2
        return jnp.einsum('bhqk,bkhd->bqhd', a.astype(v.dtype), v)

    o = _from_blocks(lax.map(block, (_to_blocks(q), starts)))
    o = _rmsnorm(o, subln) * (1.0 - lambda_init)
    return o.reshape(b, s, h * 2 * dh) @ w_o


def _sqrelu_mlp(x, w_up, w_down):
    return jnp.square(jax.nn.relu(x @ w_up)) @ w_down


def setup_inputs(seed: int = 0) -> dict:
    key = jax.random.key(seed)
    ks = jax.random.split(key, 24)
    f32 = jnp.float32
    nrm = lambda k, shape, scale: jax.random.normal(k, shape, f32) * scale
    na, nd, L = N_MLA_LAYERS, N_DIFF_LAYERS, DEPTH
    x = jax.random.normal(ks[0], (BATCH, SEQ, D_MODEL), f32)
    offsets = jax.random.randint(ks[1], (BATCH, 1), 0, 4096, dtype=jnp.int32)
    positions = (jnp.arange(SEQ, dtype=jnp.int32)[None, :] + offsets).astype(jnp.int32)
    return {
        "x": x,
        "positions": positions,
        "ln_mix_g": 1.0 + nrm(ks[2], (L, D_MODEL), 0.02),
        "ln_mix_b": nrm(ks[3], (L, D_MODEL), 0.02),
        "mla_w_dq": nrm(ks[4], (na, D_MODEL, MLA_Q_LORA), D_MODEL ** -0.5),
        "mla_q_norm": 1.0 + nrm(ks[5], (na, MLA_Q_LORA), 0.02),
        "mla_w_uq": nrm(ks[6], (na, MLA_Q_LORA, MLA_HEADS * (MLA_NOPE + MLA_ROPE)), MLA_Q_LORA ** -0.5),
        "mla_w_dkv": nrm(ks[7], (na, D_MODEL, MLA_KV_LORA + MLA_ROPE), D_MODEL ** -0.5),
        "mla_kv_norm": 1.0 + nrm(ks[8], (na, MLA_KV_LORA), 0.02),
        "mla_w_ukv": nrm(ks[9], (na, MLA_KV_LORA, MLA_HEADS * (MLA_NOPE + MLA_V)), MLA_KV_LORA ** -0.5),
        "mla_w_o": nrm(ks[10], (na, MLA_HEADS * MLA_V, D_MODEL), DEEPNORM_BETA * (MLA_HEADS * MLA_V) ** -0.5),
        "diff_w_qkv": nrm(ks[11], (nd, D_MODEL, 3 * DIFF_HEADS * 2 * DIFF_HEAD_DIM), D_MODEL ** -0.5),
        "diff_lambda_q1": nrm(ks[12], (nd, DIFF_HEAD_DIM), 0.1),
        "diff_lambda_k1": nrm(ks[13], (nd, DIFF_HEAD_DIM), 0.1),
        "diff_lambda_q2": nrm(ks[14], (nd, DIFF_HEAD_DIM), 0.1),
        "diff_lambda_k2": nrm(ks[15], (nd, DIFF_HEAD_DIM), 0.1),
        "diff_subln": 1.0 + nrm(ks[16], (nd, 2 * DIFF_HEAD_DIM), 0.02),
        "diff_w_o": nrm(ks[17], (nd, DIFF_HEADS * 2 * DIFF_HEAD_DIM, D_MODEL), DEEPNORM_BETA * D_MODEL ** -0.5),
        "ln_ffn_g": 1.0 + nrm(ks[18], (L, D_MODEL), 0.02),
        "ln_ffn_b": nrm(ks[19], (L, D_MODEL), 0.02),
        "ffn_w_up": nrm(ks[20], (L, D_MODEL, D_FF), D_MODEL ** -0.5),
        "ffn_w_down": nrm(ks[21], (L, D_FF, D_MODEL), DEEPNORM_BETA * D_FF ** -0.5),
    }


def reference(x, positions, ln_mix_g, ln_mix_b, mla_w_dq, mla_q_norm, mla_w_uq, mla_w_dkv,
              mla_kv_norm, mla_w_ukv, mla_w_o, diff_w_qkv, diff_lambda_q1, diff_lambda_k1,
              diff_lambda_q2, diff_lambda_k2, diff_subln, diff_w_o, ln_ffn_g, ln_ffn_b,
              ffn_w_up, ffn_w_down):
    for i in range(DEPTH):
        j = i // N_MIXERS
        if i % N_MIXERS == 0:
            m = _mla(x, positions, mla_w_dq[j], mla_q_norm[j], mla_w_uq[j], mla_w_dkv[j],
                     mla_kv_norm[j], mla_w_ukv[j], mla_w_o[j])
        else:
            lambda_init = 0.8 - 0.6 * math.exp(-0.3 * i)
            m = _diff_attn(x, positions, diff_w_qkv[j], diff_lambda_q1[j], diff_lambda_k1[j],
                           diff_lambda_q2[j], diff_lambda_k2[j], diff_subln[j], diff_w_o[j],
                           lambda_init)
        x = _layernorm(DEEPNORM_ALPHA * x + m, ln_mix_g[i], ln_mix_b[i])
        x = _layernorm(DEEPNORM_ALPHA * x + _sqrelu_mlp(x, ffn_w_up[i], ffn_w_down[i]),
                       ln_ffn_g[i], ln_ffn_b[i])
    return x
```

```python
import math
from contextlib import ExitStack
import numpy as np
import concourse.bass as bass
import concourse.mybir as mybir
from concourse.bass_utils import run_bass_kernel_spmd

F32 = mybir.dt.float32
BF16 = mybir.dt.bfloat16
I32 = mybir.dt.int32
AF = mybir.ActivationFunctionType
ALU = mybir.AluOpType

D = 1024
SEQ = 4096
OWN = 2048
DFF = 4096
ALPHA = 4.0 ** 0.25
LN_EPS = 1e-5
RMS_EPS = 1e-6
TWO_PI = 2.0 * math.pi
SIN_SC = 6.28318
SIN_BI = -3.14159

C_INVF, C_SSC, C_SBI, C_ROTM, C_NROTM = 0, 1, 2, 3, 4
C_LNM_G, C_LNM_B, C_LNF_G, C_LNF_B = 5, 13, 21, 29
C_X0 = 37
C_GA, C_BA = 42, 50
C_T0 = 58
NCST = 64


class Buf:
    def __init__(self, name=""):
        self.name = name
        self.w = {}
        self.r = {}
        self.sem = None
        self.tot = 0
        self.excl = False


class Sched:
    ENG = ["tensor", "vector", "scalar", "gpsimd", "sync"]

    def __init__(self, nc, stack):
        self.nc = nc
        self.stack = stack
        self.q = {e: [] for e in self.ENG}
        self.cnt = {e: 0 for e in self.ENG}
        self.esem = {e: stack.enter_context(nc.semaphore("es_" + e)) for e in self.ENG}
        self.bsem = stack.enter_context(nc.semaphore("bar"))
        self.bk = 0
        self.epoch = 0
        self.sems = {}
        self.seen = {e: {} for e in self.ENG}
        self.dbufs = []

    def semof(self, k):
        return self.esem[k[1]] if k[0] == "e" else self.sems[k]

    def _deps(self, eng, reads, writes):
        deps = []
        for b in reads:
            deps += list(b.w.items())
            if b.excl:
                deps += [(k, v) for (k, v) in b.r.items() if k != ("e", eng)]
        for b in writes:
            deps += list(b.w.items())
            deps += list(b.r.items())
        need = {}
        for (k, v) in deps:
            if k[0] == "e":
                if k[2] != self.epoch:
                    continue
                if k[1] == "tensor" and eng == "tensor":
                    continue
            if self.seen[eng].get(k, 0) < v:
                need[k] = max(need.get(k, 0), v)
        for k, v in need.items():
            self.seen[eng][k] = v
        return list(need.items())

    def op(self, eng, fn, reads=(), writes=()):
        waits = self._deps(eng, reads, writes)
        self.cnt[eng] += 1
        me = (("e", eng, self.epoch), self.cnt[eng])
        sem = self.esem[eng]
        semof = self.semof

        def run(e, waits=waits, fn=fn, sem=sem):
            for k, v in waits:
                e.wait_ge(semof(k), v)
            fn(e).then_inc(sem, 1)
        self.q[eng].append(run)
        for b in reads:
            b.r[me[0]] = max(b.r.get(me[0], 0), me[1])
        for b in writes:
            b.w = {me[0]: me[1]}
            b.r = {}
        return me

    def dma(self, eng, out, in_, dst, src=None, **kw):
        reads = [] if src is None else (list(src) if isinstance(src, (list, tuple)) else [src])
        waits = self._deps(eng, reads, [dst])
        if dst.sem is None:
            dst.sem = self.stack.enter_context(self.nc.semaphore("ds%d" % len(self.dbufs)))
            self.sems[("d", id(dst))] = dst.sem
            self.dbufs.append(dst)
        dst.tot += 16
        me = (("d", id(dst)), dst.tot)
        semof = self.semof

        def run(e, waits=waits, sem=dst.sem):
            for k, v in waits:
                e.wait_ge(semof(k), v)
            e.dma_start(out=out, in_=in_, **kw).then_inc(sem, 16)
        self.q[eng].append(run)
        for b in reads:
            b.r[me[0]] = max(b.r.get(me[0], 0), me[1])
        dst.w = {me[0]: me[1]}
        dst.r = {}
        return me

    def collective(self, fn, reads, dst):
        eng = "gpsimd"
        waits = self._deps(eng, reads, [dst])
        if dst.sem is None:
            dst.sem = self.stack.enter_context(self.nc.semaphore("cs%d" % len(self.dbufs)))
            self.sems[("d", id(dst))] = dst.sem
            self.dbufs.append(dst)
        dst.tot += 1
        me = (("d", id(dst)), dst.tot)
        semof = self.semof

        def run(e, waits=waits, sem=dst.sem):
            for k, v in waits:
                e.wait_ge(semof(k), v)
            fn(e).then_inc(sem, 1)
        self.q[eng].append(run)
        for b in reads:
            b.r[me[0]] = max(b.r.get(me[0], 0), me[1])
        dst.w = {me[0]: me[1]}
        dst.r = {}
        return me

    def wait_bufs(self, eng, bufs):
        waits = self._deps(eng, bufs, [])
        semof = self.semof

        def run(e, waits=waits):
            for k, v in waits:
                e.wait_ge(semof(k), v)
        self.q[eng].append(run)

    def barrier(self, exclude=()):
        excl = set(id(b) for b in exclude)
        targets = [(("e", f, self.epoch), self.cnt[f]) for f in self.ENG if self.cnt[f] > 0]
        targets += [(("d", id(b)), b.tot) for b in self.dbufs if id(b) not in excl]
        semof = self.semof
        self.bk += 1
        bk = self.bk
        bsem = self.bsem
        for e in self.ENG:
            waits = []
            for k, v in targets:
                if self.seen[e].get(k, 0) < v:
                    self.seen[e][k] = v
                    waits.append((k, v))

            def run(en, waits=waits, own=self.esem[e]):
                for k, v in waits:
                    en.wait_ge(semof(k), v)
                en.sem_inc(bsem, 1)
                en.wait_ge(bsem, 10 * bk - 5)
                en.sem_clear(own)
                en.sem_inc(bsem, 1)
                en.wait_ge(bsem, 10 * bk)
            self.q[e].append(run)
        self.epoch += 1
        for e in self.ENG:
            self.cnt[e] = 0

    def emit(self):
        with self.nc.Block() as block:
            for name in self.ENG:
                fns = self.q[name]

                def body(e, fns=fns):
                    for f in fns:
                        f(e)
                getattr(block, name)(body)


class Arena:
    def __init__(self, t, n):
        self.t, self.n, self.top = t, n, 0

    def f32(self, n):
        a = self.top
        self.top += n
        assert self.top <= self.n, ("arena overflow", self.top, self.n)
        return self.t[:, a:a + n]

    def bf16(self, n):
        m = (n + 1) // 2
        return self.f32(m).bitcast(BF16)[:, 0:n]

    def i32(self, n):
        return self.f32(n).bitcast(I32)


def v3(ap, c):
    return ap.rearrange("p (c t) -> p c t", c=c)


def build_fused():
    nc = bass.Bass("TRN2", target_bir_lowering=False)
    din = lambda name, shape, dt=F32: nc.dram_tensor(name, shape, dt, kind="ExternalInput").ap()
    xT_all0 = din("xT_all", [D, SEQ])
    xT_own0 = din("xT_own", [D, OWN])
    pos_alls = [din("pos_all0", [1, SEQ], I32), din("pos_all1", [1, SEQ], I32)]
    pos_own = din("pos_own", [1, OWN], I32)
    cst_ds = [din("cst0", [128, NCST]), din("cst1", [128, NCST])]
    mats_ds = [din("mats0", [128, 384]), din("mats1", [128, 384])]
    w_dq = din("w_dq", [D, 384])
    w_uq = din("w_uq", [384, 1536])
    w_dkv = din("w_dkv", [D, 384])
    w_ukv = din("w_ukv", [256, 2048])
    w_qkv = din("w_qkv", [D, 3072])
    lam_d = din("lam", [1, 256])
    w_os = [din("w_o0", [D, D]), din("w_o1", [D, D])]
    w_ups = [din("w_up0", [D, DFF]), din("w_up1", [D, DFF])]
    w_downs = [din("w_down0", [DFF, D]), din("w_down1", [DFF, D])]
    yT = nc.dram_tensor("yT", [D, OWN], F32, kind="ExternalOutput").ap()
    x1_own = nc.dram_tensor("x1_own", [D, OWN], F32).ap()
    xb_own_t = [[nc.dram_tensor("xb_own%d_%d" % (m, hf), [512, 512], BF16) for hf in range(2)] for m in range(4)]
    xb_pair_t = [[nc.dram_tensor("xb_pair%d_%d" % (m, hf), [1024, 512], BF16) for hf in range(2)] for m in range(4)]
    Bx1own = Buf("x1own")
    tabC_d = nc.dram_tensor("tabC1", [128, SEQ + OWN], BF16).ap()
    tabS_d = nc.dram_tensor("tabS1", [128, SEQ + OWN], BF16).ap()
    Btabd = Buf("tabdram")
    Bxbown = [[Buf("xbown%d_%d" % (m, hf)) for hf in range(2)] for m in range(4)]
    Bxbpair = [[Buf("xbpair%d_%d" % (m, hf)) for hf in range(2)] for m in range(4)]
    lambda_init = 0.8 - 0.6 * math.exp(-0.3 * 1)

    with ExitStack() as st:
        S = Sched(nc, st)
        NA = 51870
        arena_t = st.enter_context(nc.sbuf_tensor("arena", [128, NA], F32))
        pos_i_t = st.enter_context(nc.sbuf_tensor("pos_i", [128, 512], I32))[:, :]
        k_i_t = st.enter_context(nc.sbuf_tensor("k_i", [128, 512], I32))[:, :]
        Bposi, Bki = Buf("posi"), Buf("ki")
        A = Arena(arena_t, NA)
        ps = [st.enter_context(nc.psum_tensor("ps%d" % i, [128, 512], F32))[:, :] for i in range(8)]
        Bps = [Buf("ps%d" % i) for i in range(8)]
        for b_ in Bps:
            b_.excl = True
        bank_rr = [0]

        def nb(pool=(0, 1, 2, 3, 4, 5, 6, 7)):
            bank_rr[0] += 1
            return pool[bank_rr[0] % len(pool)]

        WQ_OFF = NA - 4096
        wq_top = v3(arena_t[:, WQ_OFF:NA].bitcast(BF16), 8)
        Bwq_top = Buf("wq_top")

        def emit_layer(L):
            A.top = 0
            cst_d, mats_d = cst_ds[L], mats_ds[L]
            pos_all = pos_alls[L]
            w_o, w_up, w_down = w_os[L], w_ups[L], w_downs[L]
            xT_own = xT_own0 if L == 0 else x1_own
            Bxres_src = None if L == 0 else Bx1own
            kmap = (lambda g: g) if L == 0 else (lambda g: (g % 2) * 16 + g // 2)

            def mm(out_ap, pairs, reads, wbuf):
                def f(e, pairs=pairs, out_ap=out_ap):
                    ins = None
                    n = len(pairs)
                    for i, (l, r) in enumerate(pairs):
                        ins = e.matmul(out_ap, lhsT=l, rhs=r, start=(i == 0), stop=(i == n - 1))
                    return ins
                S.op("tensor", f, reads=reads, writes=[wbuf])

            def vec(fn, reads, writes, eng="vector"):
                S.op(eng, fn, reads=reads, writes=writes)

            def act(out, in_, func, reads, writes, scale=None, bias=None):
                kw = {}
                if scale is not None:
                    kw["scale"] = scale
                if bias is not None:
                    kw["bias"] = bias
                S.op("scalar", lambda e: e.activation(out=out, in_=in_, func=func, **kw), reads=reads, writes=writes)

            ev_rr = [0]

            def evac(out, in_, reads, writes):
                ev_rr[0] += 1
                if ev_rr[0] % 2:
                    vec(lambda e: e.tensor_copy(out=out, in_=in_), reads, writes)
                else:
                    act(out, in_, AF.Copy, reads, writes)

            cst = A.f32(NCST)
            Bcst = Buf("cst")
            S.dma("sync", cst, cst_d, Bcst)
            mats = A.bf16(384)
            Bmats = Buf("mats")
            S.dma("gpsimd", mats, mats_d, Bmats)
            maskE, maskO, Rm = mats[:, 0:128], mats[:, 128:256], mats[:, 256:384]
            ones = A.bf16(128)
            Bones = Buf("ones")
            vec(lambda e: e.memset(ones, 1.0), [], [Bones])
            epsc = A.f32(4)
            Beps = Buf("eps")
            vec(lambda e: e.memset(epsc[:, 0:1], LN_EPS), [], [Beps])
            vec(lambda e: e.memset(epsc[:, 1:2], RMS_EPS), [], [Beps])
            vec(lambda e: e.memset(epsc[:, 2:3], SIN_BI), [], [Beps])
            vec(lambda e: e.memset(epsc[:, 3:4], 0.5), [], [Beps])
            col = lambda i: cst[:, i:i + 1]
            vec(lambda e: e.tensor_scalar(out=cst[:, C_GA:C_GA + 16], in0=cst[:, C_LNM_G:C_LNM_G + 16], scalar1=ALPHA,
                                          scalar2=None, op0=ALU.mult), [Bcst], [Bcst])
            if L == 1:
                mark_l = A.top
                lamt = A.f32(256)
                Blam = Buf("lam")
                S.dma("sync", lamt, lam_d.partition_broadcast(128), Blam)
                lamp = A.f32(128)
                lams = A.f32(4)
                vec(lambda e: e.tensor_tensor(out=lamp[:, 0:64], in0=lamt[:, 0:64], in1=lamt[:, 64:128], op=ALU.mult), [Blam], [Blam])
                vec(lambda e: e.tensor_tensor(out=lamp[:, 64:128], in0=lamt[:, 128:192], in1=lamt[:, 192:256], op=ALU.mult), [Blam], [Blam])
                vec(lambda e: e.reduce_sum(out=lams[:, 0:1], in_=lamp[:, 0:64], axis=mybir.AxisListType.X), [Blam], [Blam])
                vec(lambda e: e.reduce_sum(out=lams[:, 1:2], in_=lamp[:, 64:128], axis=mybir.AxisListType.X), [Blam], [Blam])
                act(lams[:, 2:4], lams[:, 0:2], AF.Exp, [Blam], [Blam])
                vec(lambda e: e.tensor_tensor(out=lams[:, 0:1], in0=lams[:, 3:4], in1=lams[:, 2:3], op=ALU.subtract), [Blam], [Blam])
                vec(lambda e: e.tensor_scalar(out=cst[:, C_T0 + 1:C_T0 + 2], in0=lams[:, 0:1], scalar1=-lambda_init, scalar2=None,
                                              op0=ALU.add), [Blam, Bcst], [Bcst])
                vec(lambda e: e.tensor_scalar(out=cst[:, C_T0:C_T0 + 1], in0=cst[:, C_X0:C_X0 + 1], scalar1=1.0 - lambda_init,
                                              scalar2=None, op0=ALU.mult), [Bcst], [Bcst])
                A.top = mark_l

            def table_part1(pos_sl, tmps, colf, Bc):
                pi, ki = pos_i_t, k_i_t
                pf, u, kf, Bpf, Bu, Bk = tmps
                S.dma("sync", pi, pos_sl.partition_broadcast(128), Bposi)
                vec(lambda e: e.tensor_copy(out=pf, in_=pi), [Bposi], [Bpf])
                vec(lambda e: e.tensor_scalar(out=pf, in0=pf, scalar1=colf(C_INVF), scalar2=None, op0=ALU.mult), [Bpf, Bc], [Bpf])
                vec(lambda e: e.tensor_scalar(out=u, in0=pf, scalar1=1.0 / TWO_PI, scalar2=0.5, op0=ALU.mult, op1=ALU.add), [Bpf], [Bu])
                vec(lambda e: e.tensor_copy(out=ki, in_=u), [Bu], [Bki])
                vec(lambda e: e.tensor_copy(out=kf, in_=ki), [Bki], [Bk])
                vec(lambda e: e.tensor_tensor(out=u, in0=u, in1=kf, op=ALU.subtract), [Bu, Bk], [Bu])
                vec(lambda e: e.scalar_tensor_tensor(out=kf, in0=u, scalar=0.0, in1=u, op0=ALU.is_lt, op1=ALU.add), [Bu], [Bk])
                vec(lambda e: e.tensor_scalar(out=u, in0=kf, scalar1=0.25, scalar2=None, op0=ALU.add), [Bk], [Bu])
                vec(lambda e: e.scalar_tensor_tensor(out=u, in0=u, scalar=1.0, in1=u, op0=ALU.is_ge, op1=ALU.subtract), [Bu], [Bu])

            def table_part2(Ct_sl, St_sl, Bt, tmps, colf, Bc, blend):
                pf, u, kf, Bpf, Bu, Bk = tmps
                act(St_sl, kf, AF.Sin, [Bk, Bc], [Bt], scale=colf(C_SSC), bias=colf(C_SBI))
                act(Ct_sl, u, AF.Sin, [Bu, Beps], [Bt], scale=-SIN_SC, bias=epsc[:, 2:3])
                if blend:
                    vec(lambda e: e.tensor_scalar(out=Ct_sl, in0=Ct_sl, scalar1=colf(C_ROTM), scalar2=colf(C_NROTM),
                                                  op0=ALU.mult, op1=ALU.add), [Bt, Bc], [Bt])

            def table_chunk(pos_sl, Ct_sl, St_sl, Bt, tmps, colf, Bc, blend):
                table_part1(pos_sl, tmps, colf, Bc)
                table_part2(Ct_sl, St_sl, Bt, tmps, colf, Bc, blend)

            def alloc_table_tmp():
                return (A.f32(512), A.f32(512), A.f32(512), Buf(), Buf(), Buf())

            def make_tables(pos_d, T, Ct, St, Bt):
                mark = A.top
                tmps = alloc_table_tmp()
                for c0 in range(0, T, 512):
                    sl = slice(c0, c0 + 512)
                    table_chunk(pos_d[:, sl], Ct[:, sl], St[:, sl], Bt, tmps, col, Bcst, L == 1)
                S.wait_bufs("sync", [Bposi])
                return mark

            rope_tmp = {}

            def rope(pb, Ct_sl, St_sl, Btab, outs, idx):
                rope_tmp["i"] = rope_tmp.get("i", 0) + 1
                tq, t1, t2, Bq, B1, B2 = rope_tmp["t"][rope_tmp["i"] % len(rope_tmp["t"])]
                act(tq, ps[pb], AF.Copy, [Bps[pb]], [Bq])
                rb = nb()
                mm(ps[rb], [(Rm, tq)], [Bmats, Bq], Bps[rb])
                vec(lambda e: e.tensor_tensor(out=t1, in0=ps[pb], in1=Ct_sl, op=ALU.mult), [Bps[pb], Btab], [B1])
                vec(lambda e: e.tensor_tensor(out=t2, in0=ps[rb], in1=St_sl, op=ALU.mult), [Bps[rb], Btab], [B2])
                for i, (o, Bo, psl) in enumerate(outs):
                    eng = "vector" if i == 0 else "gpsimd"
                    vec(lambda e, o=o, psl=psl: e.tensor_tensor(out=o, in0=t1[psl, :], in1=t2[psl, :], op=ALU.add), [B1, B2], [Bo], eng=eng)

            def alloc_rope_tmp(n=2):
                rope_tmp["t"] = [(A.bf16(512), A.f32(512), A.f32(512), Buf(), Buf(), Buf()) for _ in range(n)]

            def rms_tile(pbs, nfeat, gcol0, out_fn, Bout, tmp):
                sq, xf, sd, rstd, Bsq, Bxf, Bsd, Brs = tmp
                nch = len(pbs)
                for i, pb in enumerate(pbs):
                    act(sq[:, i * 512:(i + 1) * 512], ps[pb], AF.Square, [Bps[pb]], [Bsq])
                    vec(lambda e, i=i, pb=pb: e.tensor_copy(out=xf[:, i * 512:(i + 1) * 512], in_=ps[pb]), [Bps[pb]], [Bxf])
                sb = nb()
                mm(ps[sb], [(ones, sq[:, i * 512:(i + 1) * 512]) for i in range(nch)], [Bones, Bsq], Bps[sb])
                act(sd, ps[sb], AF.Ln, [Bps[sb], Beps], [Bsd], scale=1.0 / nfeat, bias=epsc[:, 1:2])
                act(rstd, sd, AF.Exp, [Bsd], [Brs], scale=-0.5)
                for i in range(nch):
                    vec(lambda e, i=i: e.scalar_tensor_tensor(out=out_fn(i), in0=xf[:, i * 512:(i + 1) * 512], scalar=col(gcol0 + i),
                                                              in1=rstd, op0=ALU.mult, op1=ALU.mult), [Bxf, Brs, Bcst], [Bout])

            def alloc_rms_tmp(nch):
                return (A.bf16(nch * 512), A.f32(nch * 512), A.f32(512), A.f32(512), Buf(), Buf(), Buf(), Buf())

            deferred = []

            def flush_deferred():
                for f_ in deferred:
                    f_()
                del deferred[:]

            def attention(m, nmaps, score_pairs, v_fn, Breads, Ptiles, post):
                sbanks = [(0, 1), (2, 3)]
                accs = [(nb((4, 6)) if nmaps == 1 else 4), None]
                if nmaps == 1:
                    ob = [accs[0]]
                    sb_ = [ob[0] + 1]
                else:
                    ob = [4, 6]
                    sb_ = [5, 7]
                ng = 8 * m + 8
                pendq = []

                def pv(items, g, c0, first, last):
                    for mp, (pt, Bpt) in enumerate(items):
                        def f(e, mp=mp, pt=pt):
                            e.matmul(ps[ob[mp]][:, c0:512], lhsT=v_fn(g), rhs=pt[:, c0:512], start=first, stop=last)
                            return e.matmul(ps[sb_[mp]][:, c0:512], lhsT=ones, rhs=pt[:, c0:512], start=first, stop=last)
                        S.op("tensor", f, reads=[Bpt, Bones] + Breads, writes=[Bps[ob[mp]], Bps[sb_[mp]]])

                for g in range(ng):
                    c0 = 0 if g < 8 * m else ((g - 8 * m) // 2) * 128
                    items = []
                    for mp in range(nmaps):
                        bk = sbanks[mp][g % 2] if nmaps == 2 else (0, 1, 2, 3)[g % 4]
                        mm(ps[bk][:, c0:512], score_pairs(mp, g, c0), Breads, Bps[bk])
                        pt, Bpt = Ptiles[(g * nmaps + mp) % len(Ptiles)]
                        act(pt[:, c0:512], ps[bk][:, c0:512], AF.Exp, [Bps[bk]], [Bpt], scale=SCALE)
                        if g >= 8 * m:
                            mk = maskE if (g - 8 * m) % 2 == 0 else maskO
                            vec(lambda e, pt=pt, mk=mk, c0=c0: e.tensor_tensor(out=pt[:, c0:c0 + 128], in0=pt[:, c0:c0 + 128], in1=mk,
                                                                               op=ALU.mult), [Bpt, Bmats], [Bpt], eng="gpsimd")
                        items.append((pt, Bpt))
                    pendq.append((items, g, c0, g == 0, g == ng - 1))
                    if len(pendq) > (1 if nmaps == 2 else 2):
                        pv(*pendq.pop(0))
                    if g == 3:
                        flush_deferred()
                while pendq:
                    pv(*pendq.pop(0))
                post(ob, sb_)

            def layernorm_tile(z, z3, Bzc, gcol, bcol, out_b, Bb, tmp, gbcol=C_LNM_G, bbcol=C_LNM_B):
                ln_stats(z, z3, Bzc, tmp)
                ln_apply(z, z3, Bzc, gcol, bcol, out_b, Bb, tmp, gbcol, bbcol)

            def ln_stats(z, z3, Bzc, tmp):
                sq, zb, mean, msq, var, sd, rstd, Bsq, Bzb, Bm, Bv, Bsd, Brs = tmp
                act(v3(sq, 8), z3, AF.Square, Bzc, [Bsq])
                b1, b2 = nb(), nb()
                mm(ps[b1], [(zb, z(c)) for c in range(8)], [Bzb] + list(Bzc), Bps[b1])
                mm(ps[b2], [(ones, sq[:, c * 512:(c + 1) * 512]) for c in range(8)], [Bones, Bsq], Bps[b2])
                vec(lambda e: e.tensor_scalar(out=mean, in0=ps[b1], scalar1=1.0 / D, scalar2=None, op0=ALU.mult), [Bps[b1]], [Bm])
                vec(lambda e: e.tensor_tensor(out=msq, in0=mean, in1=mean, op=ALU.mult), [Bm], [Bv])
                vec(lambda e: e.scalar_tensor_tensor(out=var, in0=ps[b2], scalar=1.0 / D, in1=msq, op0=ALU.mult, op1=ALU.subtract),
                    [Bps[b2], Bv], [Bv])
                act(sd, var, AF.Ln, [Bv, Beps], [Bsd], bias=epsc[:, 0:1])
                act(rstd, sd, AF.Exp, [Bsd], [Brs], scale=-0.5)

            def ln_apply(z, z3, Bzc, gcol, bcol, out_b, Bb, tmp, gbcol=C_LNM_G, bbcol=C_LNM_B):
                sq, zb, mean, msq, var, sd, rstd, Bsq, Bzb, Bm, Bv, Bsd, Brs = tmp
                for c in range(8):
                    vec(lambda e, c=c: e.tensor_tensor(out=z(c), in0=z(c), in1=mean, op=ALU.subtract), [Bzc[c], Bm], [Bzc[c]],
                        eng=("gpsimd" if c % 4 == 3 else "vector"))
                for c in range(8):
                    vec(lambda e, c=c: e.tensor_tensor(out=z(c), in0=z(c), in1=rstd, op=ALU.mult), [Bzc[c], Brs], [Bzc[c]],
                        eng=("gpsimd" if c % 2 == 0 else "vector"))
                for c in range(8):
                    if out_b is not None:
                        vec(lambda e, c=c: e.tensor_scalar(out=out_b(c), in0=z(c), scalar1=col(gbcol + c), scalar2=col(bbcol + c),
                                                           op0=ALU.mult, op1=ALU.add), [Bzc[c], Bcst], [Bb])
                for c in range(8):
                    act(z(c), z(c), AF.Identity, [Bzc[c], Bcst], [Bzc[c]], scale=col(gcol + c), bias=col(bcol + c))

            def alloc_ln_tmp(o32, Bo32):
                msq_ = A.f32(512)
                return (A.bf16(4096), o32, A.f32(512), msq_, msq_, A.f32(512), A.f32(512),
                        Buf(), Bo32, Buf(), Buf(), Buf(), Buf())


            OT = A.bf16(8 * OWN)
            OT3 = v3(OT, 8)
            BOT = [Buf("OT%d" % m) for m in range(4)]
            base_mark = A.top

            if L == 0:
                SCALE = 192.0 ** -0.5
                ckvT = A.bf16(2 * SEQ)
                ckv3 = v3(ckvT, 2)
                kpe_lo = A.bf16(SEQ)
                kpe_hi = A.bf16(SEQ)
                Bckv, Bklo, Bkhi = Buf("ckv"), Buf("klo"), Buf("khi")
                wdq = v3(A.bf16(8 * 384), 8)
                wuq = v3(A.bf16(3 * 1536), 3)
                Bwdq, Bwuq = Buf("wdq"), Buf("wuq")
                markA = A.top
                vec(lambda e: e.memset(kpe_lo, 0.0), [], [Bklo], eng="gpsimd")
                vec(lambda e: e.memset(kpe_hi, 0.0), [], [Bkhi], eng="gpsimd")
                wdkv = A.bf16(8 * 384)
                wdkv3 = v3(wdkv, 8)
                Bwdkv = Buf("wdkv")
                S.dma("gpsimd", wdkv3, w_dkv.rearrange("(c p) n -> p c n", p=128), Bwdkv)
                Call, Sall = A.f32(SEQ), A.f32(SEQ)
                Btab = Buf("tab_all")
                xa = [(v3(A.bf16(8 * 1024), 8), Buf("xa%d" % i)) for i in range(2)]

                def load_xa(t2):
                    xat, Bxa = xa[t2 % 2]
                    S.dma("gpsimd", xat, xT_all0[:, t2 * 1024:(t2 + 1) * 1024].rearrange("(c p) n -> p c n", p=128), Bxa)

                load_xa(0)
                load_xa(1)
                S.dma("gpsimd", wdq, w_dq.rearrange("(c p) n -> p c n", p=128), Bwdq)
                S.dma("gpsimd", wuq, w_uq.rearrange("(c p) n -> p c n", p=128), Bwuq)
                alloc_rope_tmp(3)
                rtmps = [alloc_rms_tmp(2), alloc_rms_tmp(2)]
                make_tables(pos_all, SEQ, Call, Sall, Btab)
                for t in range(8):
                    ts_ = slice(t * 512, (t + 1) * 512)
                    xat, Bxa = xa[(t // 2) % 2]
                    hs_ = slice((t % 2) * 512, (t % 2) * 512 + 512)
                    pbs = [nb(), nb(), nb()]
                    for oc in range(3):
                        mm(ps[pbs[oc]], [(wdkv3[:, kc, oc * 128:(oc + 1) * 128], xat[:, kc, hs_]) for kc in range(8)], [Bwdkv, Bxa], Bps[pbs[oc]])
                    if t % 2 == 1 and t // 2 + 2 < 4:
                        load_xa(t // 2 + 2)
                    rms_tile(pbs[0:2], 256.0, C_X0 + 3, lambda i, ts_=ts_: ckv3[:, i, ts_], Bckv, rtmps[t % 2])
                    rope(pbs[2], Call[:, ts_], Sall[:, ts_], Btab,
                         [(kpe_lo[0:64, ts_], Bklo, slice(0, 64)), (kpe_hi[64:128, ts_], Bkhi, slice(64, 128))], t)
                S.barrier()
                A.top = markA
                qn = A.bf16(8 * OWN)
                qn3 = v3(qn, 8)
                qr = A.bf16(4 * OWN)
                qr3 = v3(qr, 4)
                Bqn, Bqr = Buf("qn"), Buf("qr")
                markB = A.top
                Cown, Sown = A.f32(OWN), A.f32(OWN)
                Btabo = Buf("tab_own")
                tm = make_tables(pos_own, OWN, Cown, Sown, Btabo)
                A.top = tm
                S.barrier()
                cqT = v3(A.bf16(3 * 512), 3)
                Bcq = Buf("cq")
                xo = [(v3(A.bf16(8 * 1024), 8), Buf("xo%d" % i)) for i in range(2)]
                for t2 in range(2):
                    S.dma("gpsimd", xo[t2][0], xT_own[:, t2 * 1024:(t2 + 1) * 1024].rearrange("(c p) n -> p c n", p=128), xo[t2][1])
                alloc_rope_tmp(2)
                rtmps = [alloc_rms_tmp(3)] * 2
                for m in range(4):
                    ms = slice(m * 512, (m + 1) * 512)
                    xot, Bxo = xo[m // 2]
                    hs_ = slice((m % 2) * 512, (m % 2) * 512 + 512)
                    pbs = [nb(), nb(), nb()]
                    for oc in range(3):
                        mm(ps[pbs[oc]], [(wdq[:, kc, oc * 128:(oc + 1) * 128], xot[:, kc, hs_]) for kc in range(8)], [Bwdq, Bxo], Bps[pbs[oc]])
                    rms_tile(pbs, 384.0, C_X0, lambda i: cqT[:, i, :], Bcq, rtmps[m % 2])
                    for hh in range(8):
                        pb = nb()
                        mm(ps[pb], [(wuq[:, kc, hh * 128:(hh + 1) * 128], cqT[:, kc, :]) for kc in range(3)], [Bwuq, Bcq], Bps[pb])
                        evac(qn3[:, hh, ms], ps[pb], [Bps[pb]], [Bqn])
                    for pr in range(4):
                        pb = nb()
                        mm(ps[pb], [(wuq[:, kc, 1024 + pr * 128:1024 + (pr + 1) * 128], cqT[:, kc, :]) for kc in range(3)], [Bwuq, Bcq], Bps[pb])
                        rope(pb, Cown[:, ms], Sown[:, ms], Btabo, [(qr3[:, pr, ms], Bqr, slice(0, 128))], pr)
                S.barrier()
                A.top = markB
                wo_pre_off = A.top
                wo_pre = v3(A.bf16(8 * D), 8)
                wo_pre_end = A.top
                Bwo_pre = Buf("wo_pre")
                wukv = v3(A.bf16(2 * 2048), 2)
                Bwukv = Buf("wukv")
                S.dma("gpsimd", wukv, w_ukv.rearrange("(c p) n -> p c n", p=128), Bwukv)
                for c in range(8):
                    S.dma("gpsimd", wo_pre[:, c, :], w_o[c * 128:(c + 1) * 128, :], Bwo_pre)
                KT = A.bf16(SEQ)
                Vh = A.bf16(32 * 128)
                BKT, BV = Buf("KT"), Buf("V")
                Ptiles = [(A.bf16(512), Buf("P%d" % i)) for i in range(4)]
                rs_t = [(A.f32(512), Buf("rs%d" % i)) for i in range(2)]
                cst1s = A.f32(NCST)
                Bcst1 = Buf("cst1s")
                S.dma("sync", cst1s, cst_ds[1], Bcst1)
                ttmps = alloc_table_tmp()
                tstage = [(A.bf16(512), A.bf16(512), Buf("tst%d" % i)) for i in range(2)]
                tjobs = [(pos_alls[1], c0, c0) for c0 in range(0, SEQ, 512)] + [(pos_own, c0, SEQ + c0) for c0 in range(0, OWN, 512)]

                c1col = lambda i: cst1s[:, i:i + 1]

                def early_table_step(i):
                    if 1 <= i <= len(tjobs):
                        pd, c0, o0 = tjobs[i - 1]
                        Cst, Sst, Bst = tstage[(i - 1) % 2]
                        table_part2(Cst, Sst, Bst, ttmps, c1col, Bcst1, True)
                        S.dma("sync", tabC_d[:, o0:o0 + 512], Cst, Btabd, src=Bst)
                        S.dma("sync", tabS_d[:, o0:o0 + 512], Sst, Btabd, src=Bst)
                    if i < len(tjobs):
                        pd, c0, o0 = tjobs[i]
                        table_part1(pd[:, c0:c0 + 512], ttmps, c1col, Bcst1)

                for hh in range(8):
                    for t in range(8):
                        ts_ = slice(t * 512, (t + 1) * 512)
                        pb = nb()
                        mm(ps[pb], [(wukv[:, kc, hh * 128:(hh + 1) * 128], ckv3[:, kc, ts_]) for kc in range(2)], [Bwukv, Bckv], Bps[pb])
                        evac(KT[:, ts_], ps[pb], [Bps[pb]], [BKT])
                    for gb in range(8):
                        pb = nb()

                        def fv(e, gb=gb, pb=pb, hh=hh):
                            ins = None
                            for j in range(4):
                                g = gb * 4 + j
                                for kc in range(2):
                                    ins = e.matmul(ps[pb][:, j * 128:(j + 1) * 128], lhsT=ckv3[:, kc, g * 128:(g + 1) * 128],
                                                   rhs=wukv[:, kc, 1024 + hh * 128:1024 + (hh + 1) * 128], start=(kc == 0), stop=(kc == 1))
                            return ins
                        S.op("tensor", fv, reads=[Bwukv, Bckv], writes=[Bps[pb]])
                        evac(Vh[:, gb * 512:(gb + 1) * 512], ps[pb], [Bps[pb]], [BV])
                    kpe = kpe_lo if hh % 2 == 0 else kpe_hi
                    Bkpe = Bklo if hh % 2 == 0 else Bkhi
                    for m in range(4):
                        early_table_step(hh * 4 + m)

                        def sp(mp, g, c0, hh=hh, m=m, kpe=kpe):
                            gs = slice(g * 128, (g + 1) * 128)
                            return [(KT[:, gs], qn3[:, hh, m * 512 + c0:(m + 1) * 512]),
                                    (kpe[:, gs], qr3[:, hh // 2, m * 512 + c0:(m + 1) * 512])]

                        def post(ob, sb_, hh=hh, m=m):
                            rs, Brs = rs_t[m % 2]
                            act(rs, ps[sb_[0]], AF.Ln, [Bps[sb_[0]]], [Brs])
                            act(rs, rs, AF.Exp, [Brs], [Brs], scale=-1.0)
                            vec(lambda e: e.tensor_tensor(out=OT3[:, hh, m * 512:(m + 1) * 512], in0=ps[ob[0]], in1=rs, op=ALU.mult),
                                [Bps[ob[0]], Brs], [BOT[m]])
                        attention(m, 1, sp, lambda g: Vh[:, g * 128:(g + 1) * 128], [BKT, BV, Bkpe, Bqn, Bqr], Ptiles, post)
            else:
                SCALE = 64.0 ** -0.5
                xall = A.bf16(8 * SEQ)
                xall3 = v3(xall, 8)
                Bxall = Buf("xall")
                Cb, Sb = A.bf16(SEQ), A.bf16(SEQ)
                Btab = Buf("tab_all")
                qall_off = A.top
                qall = A.bf16(8 * OWN)
                qall3 = v3(qall, 8)
                Bqall2 = [Buf("qall_lo"), Buf("qall_hi")]
                wo1 = v3(qall[:, 0:8 * D], 8)
                markB = A.top
                wq, Bwq = wq_top, Bwq_top
                Cown, Sown = A.bf16(OWN), A.bf16(OWN)
                Btabo = Buf("tab_own")
                S.dma("sync", Cown, tabC_d[:, SEQ:SEQ + OWN], Btabo, src=Btabd)
                S.dma("sync", Sown, tabS_d[:, SEQ:SEQ + OWN], Btabo, src=Btabd)
                S.dma("sync", Cb, tabC_d[:, 0:SEQ], Btab, src=Btabd)
                S.dma("sync", Sb, tabS_d[:, 0:SEQ], Btab, src=Btabd)
                xo = [(v3(A.bf16(8 * 512), 8), Buf("xo%d" % i)) for i in range(2)]
                alloc_rope_tmp(3)
                assert A.top <= WQ_OFF, (A.top, WQ_OFF)
                for m in range(4):
                    ms = slice(m * 512, (m + 1) * 512)
                    xot, Bxo = xo[m % 2]
                    for hf in range(2):
                        S.dma("sync", xot[:, 4 * hf:4 * hf + 4, :], xb_own_t[m][hf].ap().rearrange("(c p) n -> p c n", p=128),
                              Bxo, src=Bxbown[m][hf])
                    for hh in range(8):
                        pb = nb()
                        mm(ps[pb], [(wq[:, kc, hh * 128:(hh + 1) * 128], xot[:, kc, :]) for kc in range(8)], [Bwq, Bxo], Bps[pb])
                        rope(pb, Cown[:, ms], Sown[:, ms], Btabo, [(qall3[:, hh, ms], Bqall2[hh // 4], slice(0, 128))], hh)
                for m in range(4):
                    for hf in range(2):
                        for r_ in range(2):
                            S.dma("sync", xall3[:, 4 * hf:4 * hf + 4, r_ * 2048 + m * 512:r_ * 2048 + (m + 1) * 512],
                                  xb_pair_t[m][hf].ap()[r_ * 512:(r_ + 1) * 512, :].rearrange("(c p) n -> p c n", p=128),
                                  Bxall, src=Bxbpair[m][hf])
                S.barrier()
                A.top = markB
                wkv = [(v3(A.bf16(8 * 128), 8), v3(A.bf16(8 * 128), 8), Buf("wk%d" % i), Buf("wv%d" % i)) for i in range(2)]
                KT = A.bf16(SEQ)
                Vh = A.bf16(32 * 128)
                BKT, BV = Buf("KT"), Buf("V")
                q1p, q2p = A.bf16(OWN), A.bf16(OWN)
                Bq1, Bq2 = Buf("q1p"), Buf("q2p")
                vec(lambda e: e.memset(q1p, 0.0), [], [Bq1], eng="gpsimd")
                vec(lambda e: e.memset(q2p, 0.0), [], [Bq2], eng="gpsimd")
                Ptiles = [(A.bf16(512), Buf("P%d" % i)) for i in range(4)]
                alloc_rope_tmp(2)
                ftmp = [(A.f32(512), Buf("f%d" % i)) for i in range(4)]
                sqo = A.bf16(512)
                Bsqo = Buf("sqo")
                def load_kv(hh):
                    wk, wv, Bwk, Bwv = wkv[hh % 2]
                    S.dma("gpsimd", wk, w_qkv[:, 1024 + hh * 128:1024 + (hh + 1) * 128].rearrange("(c p) n -> p c n", p=128), Bwk)
                    S.dma("gpsimd", wv, w_qkv[:, 2048 + hh * 128:2048 + (hh + 1) * 128].rearrange("(c p) n -> p c n", p=128), Bwv)

                load_kv(0)
                for hh in range(8):
                    wk, wv, Bwk, Bwv = wkv[hh % 2]
                    for t in range(8):
                        ts_ = slice(t * 512, (t + 1) * 512)
                        pb = nb((0, 1, 2, 3))
                        mm(ps[pb], [(wk[:, kc, :], xall3[:, kc, ts_]) for kc in range(8)], [Bwk, Bxall], Bps[pb])
                        rope(pb, Cb[:, ts_], Sb[:, ts_], Btab, [(KT[:, ts_], BKT, slice(0, 128))], t)
                        gb = t
                        pb = nb((4, 5, 6, 7))

                        def fv(e, gb=gb, pb=pb, wv=wv):
                            ins = None
                            for j in range(4):
                                g = gb * 4 + j
                                for kc in range(8):
                                    ins = e.matmul(ps[pb][:, j * 128:(j + 1) * 128], lhsT=xall3[:, kc, g * 128:(g + 1) * 128],
                                                   rhs=wv[:, kc, :], start=(kc == 0), stop=(kc == 7))
                            return ins
                        S.op("tensor", fv, reads=[Bwv, Bxall], writes=[Bps[pb]])
                        evac(Vh[:, gb * 512:(gb + 1) * 512], ps[pb], [Bps[pb]], [BV])
                    if hh + 1 < 8:
                        load_kv(hh + 1)
                    vec(lambda e, hh=hh: e.tensor_copy(out=q1p[0:64, :], in_=qall3[0:64, hh, :]), [Bqall2[hh // 4]], [Bq1])
                    vec(lambda e, hh=hh: e.tensor_copy(out=q2p[64:128, :], in_=qall3[64:128, hh, :]), [Bqall2[hh // 4]], [Bq2], eng="gpsimd")
                    if hh == 4:
                        for c in range(8):
                            S.dma("gpsimd", wo1[:, c, :], w_o[c * 128:(c + 1) * 128, :], Bqall2[0])
                    for m in range(4):
                        def sp(mp, g, c0, m=m):
                            gs = slice(kmap(g) * 128, (kmap(g) + 1) * 128)
                            qp = q1p if mp == 0 else q2p
                            return [(KT[:, gs], qp[:, m * 512 + c0:(m + 1) * 512])]

                        def post(ob, sb_, hh=hh, m=m):
                            (r1, Br1), (r2, Br2), (o1, Bo1), (o2, Bo2) = ftmp
                            sd, Bsd = r2, Br2
                            act(r1, ps[sb_[0]], AF.Ln, [Bps[sb_[0]]], [Br1])
                            act(r1, r1, AF.Exp, [Br1], [Br1], scale=-1.0)
                            act(r2, ps[sb_[1]], AF.Ln, [Bps[sb_[1]]], [Br2])
                            act(r2, r2, AF.Exp, [Br2], [Br2], scale=-1.0)
                            vec(lambda e: e.tensor_tensor(out=o1, in0=ps[ob[0]], in1=r1, op=ALU.mult), [Bps[ob[0]], Br1], [Bo1])
                            vec(lambda e: e.tensor_tensor(out=o2, in0=ps[ob[1]], in1=r2, op=ALU.mult), [Bps[ob[1]], Br2], [Bo2])
                            vec(lambda e: e.scalar_tensor_tensor(out=o1, in0=o2, scalar=col(C_T0 + 1), in1=o1, op0=ALU.mult, op1=ALU.add),
                                [Bo1, Bo2, Bcst], [Bo1])

                            def tail(hh=hh, m=m):
                                act(sqo, o1, AF.Square, [Bo1], [Bsqo])
                                pb = nb((0, 1, 2, 3))
                                mm(ps[pb], [(ones, sqo)], [Bones, Bsqo], Bps[pb])
                                act(sd, ps[pb], AF.Ln, [Bps[pb], Beps], [Bsd], scale=1.0 / 128.0, bias=epsc[:, 1:2])
                                act(r1, sd, AF.Exp, [Bsd], [Br1], scale=-0.5)
                                vec(lambda e: e.scalar_tensor_tensor(out=OT3[:, hh, m * 512:(m + 1) * 512], in0=o1, scalar=col(C_T0), in1=r1,
                                                                     op0=ALU.mult, op1=ALU.mult), [Bo1, Br1, Bcst], [BOT[m]])
                            deferred.append(tail)
                        attention(m, 2, sp, lambda g: Vh[:, kmap(g) * 128:(kmap(g) + 1) * 128], [BKT, BV, Bq1, Bq2], Ptiles, post)

            flush_deferred()
            S.barrier()
            A.top = base_mark
            resid = A.f32(8 * OWN)
            resid3 = v3(resid, 8)
            Bres = [[Buf("res%d_%d" % (m, c)) for c in range(8)] for m in range(4)]
            slot0_off = A.top
            wu0 = v3(A.bf16(8 * 1024), 8)
            Bwu0 = Buf("wu0")
            markD = A.top
            if L == 0:
                assert A.top <= wo_pre_off, (A.top, wo_pre_off)
                A.top = wo_pre_end
                wo, Bwo = wo_pre, Bwo_pre
            else:
                assert A.top == qall_off, (A.top, qall_off)
                A.top += 4096
                wo, Bwo = wo1, Bqall2[0]
            Bxload = [Buf("xload%d" % i) for i in range(3)]

            def xload(i_):
                a_, b_ = [(0, 512), (512, 1024), (1024, 2048)][i_]
                S.dma("sync", resid3[:, :, a_:b_], xT_own[:, a_:b_].rearrange("(c p) n -> p c n", p=128), Bxload[i_], src=Bxres_src)

            xload(0)
            o32 = A.f32(128)
            Bo32 = Buf("ones32")
            vec(lambda e: e.memset(o32, 1.0), [], [Bo32])
            lnts = [alloc_ln_tmp(o32, Bo32), alloc_ln_tmp(o32, Bo32)]
            def zargs(m):
                ms = slice(m * 512, (m + 1) * 512)
                return (lambda c, ms=ms: resid3[:, c, ms]), resid3[:, :, ms], Bres[m]

            pend_ln = None
            for m in range(4):
                ms = slice(m * 512, (m + 1) * 512)
                if m == 1:
                    xload(1)
                    xload(2)
                    for c in range(8):
                        S.dma("gpsimd", wu0[:, c, :], w_up[c * 128:(c + 1) * 128, 0:1024], Bwu0)
                for dc in range(8):
                    pb = nb()
                    mm(ps[pb], [(wo[:, hh, dc * 128:(dc + 1) * 128], OT3[:, hh, ms]) for hh in range(8)], [Bwo, BOT[m]], Bps[pb])
                    vec(lambda e, dc=dc, pb=pb, ms=ms: e.scalar_tensor_tensor(out=resid3[:, dc, ms], in0=resid3[:, dc, ms], scalar=ALPHA,
                                                                              in1=ps[pb], op0=ALU.mult, op1=ALU.add),
                        [Bxload[min(m, 2)], Bps[pb]], [Bres[m][dc]])
                z, z3, Bzc = zargs(m)
                ln_stats(z, z3, Bzc, lnts[m % 2])
                if pend_ln is not None:
                    pm = pend_ln
                    z, z3, Bzc = zargs(pm)
                    ln_apply(z, z3, Bzc, C_GA, C_BA, lambda c, pm=pm: OT3[:, c, pm * 512:(pm + 1) * 512], BOT[pm], lnts[pm % 2])
                pend_ln = m
            pm = pend_ln
            z, z3, Bzc = zargs(pm)
            ln_apply(z, z3, Bzc, C_GA, C_BA, lambda c, pm=pm: OT3[:, c, pm * 512:(pm + 1) * 512], BOT[pm], lnts[pm % 2])
            S.barrier()
            A.top = markD
            wslots = [(wu0, v3(A.bf16(8 * 1024), 8), Bwu0, Buf("wd0")),
                      (v3(A.bf16(8 * 1024), 8), v3(A.bf16(8 * 1024), 8), Buf("wu1"), Buf("wd1"))]
            hTs = [(v3(A.bf16(8 * 512), 8), Buf("hT%d" % i)) for i in range(2)]
            rts = [(A.f32(512), Buf("r%d" % i)) for i in range(3)]
            rr = [0]

            def up_proj(fq, m, slot):
                wu, wd, Bwu, Bwd = wslots[slot]
                hT, BhT = hTs[m % 2]
                ms = slice(m * 512, (m + 1) * 512)
                for fc in range(8):
                    pb = nb()
                    mm(ps[pb], [(wu[:, kc, fc * 128:(fc + 1) * 128], OT3[:, kc, ms]) for kc in range(8)], [Bwu, BOT[m]], Bps[pb])
                    rr[0] += 1
                    r, Br = rts[rr[0] % 3]
                    act(r, ps[pb], AF.Relu, [Bps[pb]], [Br])
                    vec(lambda e, r=r, fc=fc, hT=hT: e.tensor_tensor(out=hT[:, fc, :], in0=r, in1=r, op=ALU.mult), [Br], [BhT])

            def down_proj(fq, m, slot):
                wu, wd, Bwu, Bwd = wslots[slot]
                hT, BhT = hTs[m % 2]
                ms = slice(m * 512, (m + 1) * 512)
                for dc in range(8):
                    pb = nb()
                    mm(ps[pb], [(wd[:, fc, dc * 128:(dc + 1) * 128], hT[:, fc, :]) for fc in range(8)], [Bwd, BhT], Bps[pb])
                    vec(lambda e, dc=dc, pb=pb, ms=ms: e.tensor_tensor(out=resid3[:, dc, ms], in0=resid3[:, dc, ms], in1=ps[pb], op=ALU.add),
                        [Bps[pb], Bres[m][dc]], [Bres[m][dc]])

            def load_w(fq):
                wu, wd, Bwu, Bwd = wslots[fq % 2]
                for c in range(8 if fq > 0 else 0):
                    S.dma("gpsimd", wu[:, c, :], w_up[c * 128:(c + 1) * 128, fq * 1024:(fq + 1) * 1024], Bwu)
                for c in range(8):
                    S.dma("gpsimd", wd[:, c, :], w_down[fq * 1024 + c * 128:fq * 1024 + (c + 1) * 128, :], Bwd)

            def _f(o, n):
                return arena_t[:, slot0_off + o:slot0_off + o + n]
            sq_ = _f(0, 2048).bitcast(BF16)
            zb_ = _f(2048, 128)
            Bsq_, Bzb_ = Buf("sq2"), Buf("ones32b")
            lnts = []
            for i_ in range(2):
                o_ = 4096 + i_ * 2048
                msq_ = _f(o_ + 512, 512)
                lnts.append((sq_, zb_, _f(o_, 512), msq_, msq_, _f(o_ + 1024, 512), _f(o_ + 1536, 512),
                             Bsq_, Bzb_, Buf(), Buf(), Buf(), Buf()))

            def ln2_fence():
                allb = [Bsq_, Bzb_] + [b_ for t_ in lnts for b_ in t_[9:]]
                act(sq_[:, 0:8], sq_[:, 8:16], AF.Copy, [], [wslots[0][2], wslots[0][3]] + allb)
                vec(lambda e: e.memset(zb_, 1.0), [], [Bzb_])
            By = Buf("y")
            pend_cc = None

            def emit_cc(m_):
                for hf in range(2):
                    S.collective(lambda e, m_=m_, hf=hf: e.collective_compute("AllGather", ALU.bypass,
                                                                               replica_groups=[[0, 1], [2, 3], [4, 5], [6, 7]],
                                                                               ins=[xb_own_t[m_][hf].ap().opt()],
                                                                               outs=[xb_pair_t[m_][hf].ap().opt()]),
                                 [Bxbown[m_][hf]], Bxbpair[m_][hf])

            def finish_tile(m):
                nonlocal pend_cc
                ms = slice(m * 512, (m + 1) * 512)
                z, z3, Bzc = zargs(m)
                if L == 0:
                    ln_apply(z, z3, Bzc, C_LNF_G, C_LNF_B, lambda c, ms=ms: OT3[:, c, ms], BOT[m], lnts[m % 2], C_LNF_G, C_LNF_B)
                    S.dma("sync", x1_own[:, ms].rearrange("(c p) n -> p c n", p=128), resid3[:, :, ms], Bx1own, src=Bres[m])
                    for hf in range(2):
                        S.dma("sync", xb_own_t[m][hf].ap().rearrange("(c p) n -> p c n", p=128), OT3[:, 4 * hf:4 * hf + 4, ms],
                              Bxbown[m][hf], src=BOT[m])
                    if pend_cc is not None:
                        emit_cc(pend_cc)
                    pend_cc = m
                else:
                    ln_apply(z, z3, Bzc, C_LNF_G, C_LNF_B, None, None, lnts[m % 2])
                    S.dma("sync", yT[:, ms].rearrange("(c p) n -> p c n", p=128), resid3[:, :, ms], By, src=Bres[m])


            def ln2_step(m):
                z, z3, Bzc = zargs(m)
                ln_stats(z, z3, Bzc, lnts[m % 2])
                if m > 0:
                    finish_tile(m - 1)

            pend = None
            load_w(0)
            for fq in range(4):
                slot = fq % 2
                for m in range(4):
                    up_proj(fq, m, slot)
                    if pend is not None:
                        down_proj(*pend)
                        if pend[0] == 3:
                            ln2_step(pend[1])
                    pend = (fq, m, slot)
                    if m == 0 and fq + 1 < 4:
                        load_w(fq + 1)
                    if L == 0 and fq == 2 and m == 1:
                        assert A.top <= WQ_OFF, (A.top, WQ_OFF)
                        S.dma("gpsimd", wq_top, w_qkv[:, 0:1024].rearrange("(c p) n -> p c n", p=128), Bwq_top)
                    if fq == 3 and m == 0:
                        ln2_fence()
            down_proj(*pend)
            ln2_step(3)
            finish_tile(3)
            if L == 0:
                emit_cc(pend_cc)
                return
            S.wait_bufs("sync", [By])
            S.barrier()

        emit_layer(0)
        S.barrier(exclude=[b_ for row in Bxbpair for b_ in row])
        emit_layer(1)
        S.emit()
    return nc


def _own_idx(h):
    return np.concatenate([np.arange(128) + 128 * (2 * j + h) for j in range(16)])


def _cols(v):
    return np.ascontiguousarray(v.reshape(-1, 128).T)


def _consts(L, inp):
    cst = np.zeros((128, NCST), np.float32)
    p = np.arange(128)
    if L == 0:
        i = p % 32
        invf = (np.float32(10000.0) ** (-(2 * i).astype(np.float32) / np.float32(64))).astype(np.float32)
        sign = np.where(p % 64 < 32, -1.0, 1.0)
        rot = np.ones(128)
        partner = np.where(p % 64 < 32, p + 32, p - 32)
    else:
        i = p % 8
        invf = (np.float32(500000.0) ** (-(2 * i).astype(np.float32) / np.float32(16))).astype(np.float32)
        rot = (p % 64 < 16).astype(np.float64)
        invf = invf * rot
        sign = np.where(p % 64 < 8, -1.0, 1.0) * rot
        partner = np.where(p % 64 < 8, p + 8, p - 8)
    cst[:, C_INVF] = invf
    cst[:, C_SSC] = sign * SIN_SC
    cst[:, C_SBI] = sign * SIN_BI
    cst[:, C_ROTM] = rot
    cst[:, C_NROTM] = 1.0 - rot
    cst[:, C_LNM_G:C_LNM_G + 8] = _cols(inp["ln_mix_g"][L])
    cst[:, C_LNM_B:C_LNM_B + 8] = _cols(inp["ln_mix_b"][L])
    cst[:, C_LNF_G:C_LNF_G + 8] = _cols(inp["ln_ffn_g"][L])
    cst[:, C_LNF_B:C_LNF_B + 8] = _cols(inp["ln_ffn_b"][L])
    if L == 0:
        cst[:, C_X0:C_X0 + 3] = _cols(inp["mla_q_norm"][0])
        cst[:, C_X0 + 3:C_X0 + 5] = _cols(inp["mla_kv_norm"][0])
    else:
        cst[:, C_X0] = inp["diff_subln"][0]
    R = np.zeros((128, 128), np.float32)
    for m_ in range(128):
        if rot[m_] > 0:
            R[partner[m_], m_] = 1.0
    return cst, R


def _masks(h):
    k = np.arange(128)[:, None]
    q = np.arange(128)[None, :]
    tri = (k <= q).astype(np.float32)
    if h == 0:
        return tri, np.zeros((128, 128), np.float32)
    return np.ones((128, 128), np.float32), tri


def _layer_weights(L, inp):
    w = {}
    if L == 0:
        w["w_dq"] = np.ascontiguousarray(inp["mla_w_dq"][0])
        uq = inp["mla_w_uq"][0].reshape(384, 8, 192)
        w["w_uq"] = np.ascontiguousarray(np.concatenate([uq[:, :, :128].reshape(384, 1024), uq[:, :, 128:].reshape(384, 512)], axis=1))
        dkv = inp["mla_w_dkv"][0]
        w["w_dkv"] = np.ascontiguousarray(np.concatenate([dkv, dkv[:, 256:320]], axis=1))
        ukv = inp["mla_w_ukv"][0].reshape(256, 8, 256)
        w["w_ukv"] = np.ascontiguousarray(np.concatenate([ukv[:, :, :128].reshape(256, 1024), ukv[:, :, 128:].reshape(256, 1024)], axis=1))
        w["w_o"] = np.ascontiguousarray(inp["mla_w_o"][0])
    else:
        w["w_qkv"] = np.ascontiguousarray(inp["diff_w_qkv"][0])
        w["lam"] = np.ascontiguousarray(np.concatenate([inp["diff_lambda_q1"][0], inp["diff_lambda_k1"][0],
                                                         inp["diff_lambda_q2"][0], inp["diff_lambda_k2"][0]]).reshape(1, 256))
        w["w_o"] = np.ascontiguousarray(inp["diff_w_o"][0])
    w["w_up"] = np.ascontiguousarray(inp["ffn_w_up"][L])
    w["w_down"] = np.ascontiguousarray(inp["ffn_w_down"][L])
    return w


_NC_CACHE = {}


def kernel(**inputs):
    inp = {k: np.asarray(v) for k, v in inputs.items()}
    x = inp["x"].astype(np.float32, copy=False)
    positions = inp["positions"]
    if "nc" not in _NC_CACHE:
        _NC_CACHE["nc"] = build_fused()
    nc = _NC_CACHE["nc"]
    cst0, R0 = _consts(0, inp)
    cst1, R1 = _consts(1, inp)
    w0 = _layer_weights(0, inp)
    w1 = _layer_weights(1, inp)
    shared = {
        "cst0": cst0, "cst1": cst1,
        "w_dq": w0["w_dq"], "w_uq": w0["w_uq"], "w_dkv": w0["w_dkv"], "w_ukv": w0["w_ukv"],
        "w_qkv": w1["w_qkv"], "lam": w1["lam"],
        "w_o0": w0["w_o"], "w_o1": w1["w_o"], "w_up0": w0["w_up"], "w_up1": w1["w_up"],
        "w_down0": w0["w_down"], "w_down1": w1["w_down"],
    }
    perm = np.concatenate([_own_idx(0), _own_idx(1)])
    in_maps = []
    for c in range(8):
        b, h = c // 2, c % 2
        own = _own_idx(h)
        xT = np.ascontiguousarray(x[b].T)
        mE, mO = _masks(h)
        pos = positions[b].astype(np.int32)
        d = {
            "xT_all": xT,
            "xT_own": np.ascontiguousarray(xT[:, own]),
            "pos_all0": np.ascontiguousarray(pos.reshape(1, SEQ)),
            "pos_all1": np.ascontiguousarray(pos[perm].reshape(1, SEQ)),
            "pos_own": np.ascontiguousarray(pos[own].reshape(1, OWN)),
            "mats0": np.ascontiguousarray(np.concatenate([mE, mO, R0], axis=1)),
            "mats1": np.ascontiguousarray(np.concatenate([mE, mO, R1], axis=1)),
        }
        d.update(shared)
        in_maps.append(d)
    res = run_bass_kernel_spmd(nc, in_maps, core_ids=list(range(8)))
    out = np.empty((4, SEQ, D), np.float32)
    for c in range(8):
        b, h = c // 2, c % 2
        out[b, _own_idx(h), :] = res.results[c]["yT"].T
    return out
```

```python
import math
from contextlib import ExitStack
import numpy as np
import concourse.bass as bass
import concourse.mybir as mybir
from concourse.bass_utils import run_bass_kernel_spmd

F32 = mybir.dt.float32
BF16 = mybir.dt.bfloat16
I32 = mybir.dt.int32
AF = mybir.ActivationFunctionType
ALU = mybir.AluOpType

D = 1024
SEQ = 4096
OWN = 2048
DFF = 4096
ALPHA = 4.0 ** 0.25
LN_EPS = 1e-5
RMS_EPS = 1e-6
TWO_PI = 2.0 * math.pi
SIN_SC = 6.28318
SIN_BI = -3.14159

C_INVF, C_SSC, C_SBI, C_ROTM, C_NROTM = 0, 1, 2, 3, 4
C_LNM_G, C_LNM_B, C_LNF_G, C_LNF_B = 5, 13, 21, 29
C_X0 = 37
C_GA, C_BA = 42, 50
C_T0 = 58
NCST = 64


class Buf:
    def __init__(self, name=""):
        self.name = name
        self.w = {}
        self.r = {}
        self.sem = None
        self.tot = 0
        self.excl = False


class Sched:
    ENG = ["tensor", "vector", "scalar", "gpsimd", "sync"]

    def __init__(self, nc, stack):
        self.nc = nc
        self.stack = stack
        self.q = {e: [] for e in self.ENG}
        self.cnt = {e: 0 for e in self.ENG}
        self.esem = {e: stack.enter_context(nc.semaphore("es_" + e)) for e in self.ENG}
        self.bsem = stack.enter_context(nc.semaphore("bar"))
        self.bk = 0
        self.epoch = 0
        self.sems = {}
        self.seen = {e: {} for e in self.ENG}
        self.dbufs = []

    def semof(self, k):
        return self.esem[k[1]] if k[0] == "e" else self.sems[k]

    def _deps(self, eng, reads, writes):
        deps = []
        for b in reads:
            deps += list(b.w.items())
            if b.excl:
                deps += [(k, v) for (k, v) in b.r.items() if k != ("e", eng)]
        for b in writes:
            deps += list(b.w.items())
            deps += list(b.r.items())
        need = {}
        for (k, v) in deps:
            if k[0] == "e":
                if k[2] != self.epoch:
                    continue
                if k[1] == "tensor" and eng == "tensor":
                    continue
            if self.seen[eng].get(k, 0) < v:
                need[k] = max(need.get(k, 0), v)
        for k, v in need.items():
            self.seen[eng][k] = v
        return list(need.items())

    def op(self, eng, fn, reads=(), writes=()):
        waits = self._deps(eng, reads, writes)
        self.cnt[eng] += 1
        me = (("e", eng, self.epoch), self.cnt[eng])
        sem = self.esem[eng]
        semof = self.semof

        def run(e, waits=waits, fn=fn, sem=sem):
            for k, v in waits:
                e.wait_ge(semof(k), v)
            fn(e).then_inc(sem, 1)
        self.q[eng].append(run)
        for b in reads:
            b.r[me[0]] = max(b.r.get(me[0], 0), me[1])
        for b in writes:
            b.w = {me[0]: me[1]}
            b.r = {}
        return me

    def dma(self, eng, out, in_, dst, src=None, **kw):
        reads = [] if src is None else (list(src) if isinstance(src, (list, tuple)) else [src])
        waits = self._deps(eng, reads, [dst])
        if dst.sem is None:
            dst.sem = self.stack.enter_context(self.nc.semaphore("ds%d" % len(self.dbufs)))
            self.sems[("d", id(dst))] = dst.sem
            self.dbufs.append(dst)
        dst.tot += 16
        me = (("d", id(dst)), dst.tot)
        semof = self.semof

        def run(e, waits=waits, sem=dst.sem):
            for k, v in waits:
                e.wait_ge(semof(k), v)
            e.dma_start(out=out, in_=in_, **kw).then_inc(sem, 16)
        self.q[eng].append(run)
        for b in reads:
            b.r[me[0]] = max(b.r.get(me[0], 0), me[1])
        dst.w = {me[0]: me[1]}
        dst.r = {}
        return me

    def collective(self, fn, reads, dst):
        eng = "gpsimd"
        waits = self._deps(eng, reads, [dst])
        if dst.sem is None:
            dst.sem = self.stack.enter_context(self.nc.semaphore("cs%d" % len(self.dbufs)))
            self.sems[("d", id(dst))] = dst.sem
            self.dbufs.append(dst)
        dst.tot += 1
        me = (("d", id(dst)), dst.tot)
        semof = self.semof

        def run(e, waits=waits, sem=dst.sem):
            for k, v in waits:
                e.wait_ge(semof(k), v)
            fn(e).then_inc(sem, 1)
        self.q[eng].append(run)
        for b in reads:
            b.r[me[0]] = max(b.r.get(me[0], 0), me[1])
        dst.w = {me[0]: me[1]}
        dst.r = {}
        return me

    def wait_bufs(self, eng, bufs):
        waits = self._deps(eng, bufs, [])
        semof = self.semof

        def run(e, waits=waits):
            for k, v in waits:
                e.wait_ge(semof(k), v)
        self.q[eng].append(run)

    def barrier(self, exclude=()):
        excl = set(id(b) for b in exclude)
        targets = [(("e", f, self.epoch), self.cnt[f]) for f in self.ENG if self.cnt[f] > 0]
        targets += [(("d", id(b)), b.tot) for b in self.dbufs if id(b) not in excl]
        semof = self.semof
        self.bk += 1
        bk = self.bk
        bsem = self.bsem
        for e in self.ENG:
            waits = []
            for k, v in targets:
                if self.seen[e].get(k, 0) < v:
                    self.seen[e][k] = v
                    waits.append((k, v))

            def run(en, waits=waits, own=self.esem[e]):
                for k, v in waits:
                    en.wait_ge(semof(k), v)
                en.sem_inc(bsem, 1)
                en.wait_ge(bsem, 10 * bk - 5)
                en.sem_clear(own)
                en.sem_inc(bsem, 1)
                en.wait_ge(bsem, 10 * bk)
            self.q[e].append(run)
        self.epoch += 1
        for e in self.ENG:
            self.cnt[e] = 0

    def emit(self):
        with self.nc.Block() as block:
            for name in self.ENG:
                fns = self.q[name]

                def body(e, fns=fns):
                    for f in fns:
                        f(e)
                getattr(block, name)(body)


class Arena:
    def __init__(self, t, n):
        self.t, self.n, self.top = t, n, 0

    def f32(self, n):
        a = self.top
        self.top += n
        assert self.top <= self.n, ("arena overflow", self.top, self.n)
        return self.t[:, a:a + n]

    def bf16(self, n):
        m = (n + 1) // 2
        return self.f32(m).bitcast(BF16)[:, 0:n]

    def i32(self, n):
        return self.f32(n).bitcast(I32)


def v3(ap, c):
    return ap.rearrange("p (c t) -> p c t", c=c)


def build_fused():
    nc = bass.Bass("TRN2", target_bir_lowering=False)
    din = lambda name, shape, dt=F32: nc.dram_tensor(name, shape, dt, kind="ExternalInput").ap()
    xT_all0 = din("xT_all", [D, SEQ])
    xT_own0 = din("xT_own", [D, OWN])
    pos_alls = [din("pos_all0", [1, SEQ], I32), din("pos_all1", [1, SEQ], I32)]
    pos_own = din("pos_own", [1, OWN], I32)
    cst_ds = [din("cst0", [128, NCST]), din("cst1", [128, NCST])]
    mats_ds = [din("mats0", [128, 384]), din("mats1", [128, 384])]
    w_dq = din("w_dq", [D, 384])
    w_uq = din("w_uq", [384, 1536])
    w_dkv = din("w_dkv", [D, 384])
    w_ukv = din("w_ukv", [256, 2048])
    w_qkv = din("w_qkv", [D, 3072])
    lam_d = din("lam", [1, 256])
    w_os = [din("w_o0", [D, D]), din("w_o1", [D, D])]
    w_ups = [din("w_up0", [D, DFF]), din("w_up1", [D, DFF])]
    w_downs = [din("w_down0", [DFF, D]), din("w_down1", [DFF, D])]
    yT = nc.dram_tensor("yT", [D, OWN], F32, kind="ExternalOutput").ap()
    x1_own = nc.dram_tensor("x1_own", [D, OWN], F32).ap()
    xb_own_t = [[nc.dram_tensor("xb_own%d_%d" % (m, hf), [512, 512], BF16) for hf in range(2)] for m in range(4)]
    xb_pair_t = [[nc.dram_tensor("xb_pair%d_%d" % (m, hf), [1024, 512], BF16) for hf in range(2)] for m in range(4)]
    Bx1own = Buf("x1own")
    tabC_d = nc.dram_tensor("tabC1", [128, SEQ + OWN], BF16).ap()
    tabS_d = nc.dram_tensor("tabS1", [128, SEQ + OWN], BF16).ap()
    Btabd = Buf("tabdram")
    Bxbown = [[Buf("xbown%d_%d" % (m, hf)) for hf in range(2)] for m in range(4)]
    Bxbpair = [[Buf("xbpair%d_%d" % (m, hf)) for hf in range(2)] for m in range(4)]
    lambda_init = 0.8 - 0.6 * math.exp(-0.3 * 1)

    with ExitStack() as st:
        S = Sched(nc, st)
        NA = 51870
        arena_t = st.enter_context(nc.sbuf_tensor("arena", [128, NA], F32))
        pos_i_t = st.enter_context(nc.sbuf_tensor("pos_i", [128, 512], I32))[:, :]
        k_i_t = st.enter_context(nc.sbuf_tensor("k_i", [128, 512], I32))[:, :]
        Bposi, Bki = Buf("posi"), Buf("ki")
        A = Arena(arena_t, NA)
        ps = [st.enter_context(nc.psum_tensor("ps%d" % i, [128, 512], F32))[:, :] for i in range(8)]
        Bps = [Buf("ps%d" % i) for i in range(8)]
        for b_ in Bps:
            b_.excl = True
        bank_rr = [0]

        def nb(pool=(0, 1, 2, 3, 4, 5, 6, 7)):
            bank_rr[0] += 1
            return pool[bank_rr[0] % len(pool)]

        WQ_OFF = NA - 4096
        wq_top = v3(arena_t[:, WQ_OFF:NA].bitcast(BF16), 8)
        Bwq_top = Buf("wq_top")

        def emit_layer(L):
            A.top = 0
            cst_d, mats_d = cst_ds[L], mats_ds[L]
            pos_all = pos_alls[L]
            w_o, w_up, w_down = w_os[L], w_ups[L], w_downs[L]
            xT_own = xT_own0 if L == 0 else x1_own
            Bxres_src = None if L == 0 else Bx1own
            kmap = (lambda g: g) if L == 0 else (lambda g: (g % 2) * 16 + g // 2)

            def mm(out_ap, pairs, reads, wbuf):
                def f(e, pairs=pairs, out_ap=out_ap):
                    ins = None
                    n = len(pairs)
                    for i, (l, r) in enumerate(pairs):
                        ins = e.matmul(out_ap, lhsT=l, rhs=r, start=(i == 0), stop=(i == n - 1))
                    return ins
                S.op("tensor", f, reads=reads, writes=[wbuf])

            def vec(fn, reads, writes, eng="vector"):
                S.op(eng, fn, reads=reads, writes=writes)

            def act(out, in_, func, reads, writes, scale=None, bias=None):
                kw = {}
                if scale is not None:
                    kw["scale"] = scale
                if bias is not None:
                    kw["bias"] = bias
                S.op("scalar", lambda e: e.activation(out=out, in_=in_, func=func, **kw), reads=reads, writes=writes)

            ev_rr = [0]

            def evac(out, in_, reads, writes):
                ev_rr[0] += 1
                if ev_rr[0] % 2:
                    vec(lambda e: e.tensor_copy(out=out, in_=in_), reads, writes)
                else:
                    act(out, in_, AF.Copy, reads, writes)

            cst = A.f32(NCST)
            Bcst = Buf("cst")
            S.dma("sync", cst, cst_d, Bcst)
            mats = A.bf16(384)
            Bmats = Buf("mats")
            S.dma("gpsimd", mats, mats_d, Bmats)
            maskE, maskO, Rm = mats[:, 0:128], mats[:, 128:256], mats[:, 256:384]
            ones = A.bf16(128)
            Bones = Buf("ones")
            vec(lambda e: e.memset(ones, 1.0), [], [Bones])
            epsc = A.f32(4)
            Beps = Buf("eps")
            vec(lambda e: e.memset(epsc[:, 0:1], LN_EPS), [], [Beps])
            vec(lambda e: e.memset(epsc[:, 1:2], RMS_EPS), [], [Beps])
            vec(lambda e: e.memset(epsc[:, 2:3], SIN_BI), [], [Beps])
            vec(lambda e: e.memset(epsc[:, 3:4], 0.5), [], [Beps])
            col = lambda i: cst[:, i:i + 1]
            vec(lambda e: e.tensor_scalar(out=cst[:, C_GA:C_GA + 16], in0=cst[:, C_LNM_G:C_LNM_G + 16], scalar1=ALPHA,
                                          scalar2=None, op0=ALU.mult), [Bcst], [Bcst])
            if L == 1:
                mark_l = A.top
                lamt = A.f32(256)
                Blam = Buf("lam")
                S.dma("sync", lamt, lam_d.partition_broadcast(128), Blam)
                lamp = A.f32(128)
                lams = A.f32(4)
                vec(lambda e: e.tensor_tensor(out=lamp[:, 0:64], in0=lamt[:, 0:64], in1=lamt[:, 64:128], op=ALU.mult), [Blam], [Blam])
                vec(lambda e: e.tensor_tensor(out=lamp[:, 64:128], in0=lamt[:, 128:192], in1=lamt[:, 192:256], op=ALU.mult), [Blam], [Blam])
                vec(lambda e: e.reduce_sum(out=lams[:, 0:1], in_=lamp[:, 0:64], axis=mybir.AxisListType.X), [Blam], [Blam])
                vec(lambda e: e.reduce_sum(out=lams[:, 1:2], in_=lamp[:, 64:128], axis=mybir.AxisListType.X), [Blam], [Blam])
                act(lams[:, 2:4], lams[:, 0:2], AF.Exp, [Blam], [Blam])
                vec(lambda e: e.tensor_tensor(out=lams[:, 0:1], in0=lams[:, 3:4], in1=lams[:, 2:3], op=ALU.subtract), [Blam], [Blam])
                vec(lambda e: e.tensor_scalar(out=cst[:, C_T0 + 1:C_T0 + 2], in0=lams[:, 0:1], scalar1=-lambda_init, scalar2=None,
                                              op0=ALU.add), [Blam, Bcst], [Bcst])
                vec(lambda e: e.tensor_scalar(out=cst[:, C_T0:C_T0 + 1], in0=cst[:, C_X0:C_X0 + 1], scalar1=1.0 - lambda_init,
                                              scalar2=None, op0=ALU.mult), [Bcst], [Bcst])
                A.top = mark_l

            def table_part1(pos_sl, tmps, colf, Bc):
                pi, ki = pos_i_t, k_i_t
                pf, u, kf, Bpf, Bu, Bk = tmps
                S.dma("sync", pi, pos_sl.partition_broadcast(128), Bposi)
                vec(lambda e: e.tensor_copy(out=pf, in_=pi), [Bposi], [Bpf])
                vec(lambda e: e.tensor_scalar(out=pf, in0=pf, scalar1=colf(C_INVF), scalar2=None, op0=ALU.mult), [Bpf, Bc], [Bpf])
                vec(lambda e: e.tensor_scalar(out=u, in0=pf, scalar1=1.0 / TWO_PI, scalar2=0.5, op0=ALU.mult, op1=ALU.add), [Bpf], [Bu])
                vec(lambda e: e.tensor_copy(out=ki, in_=u), [Bu], [Bki])
                vec(lambda e: e.tensor_copy(out=kf, in_=ki), [Bki], [Bk])
                vec(lambda e: e.tensor_tensor(out=u, in0=u, in1=kf, op=ALU.subtract), [Bu, Bk], [Bu])
                vec(lambda e: e.scalar_tensor_tensor(out=kf, in0=u, scalar=0.0, in1=u, op0=ALU.is_lt, op1=ALU.add), [Bu], [Bk])
                vec(lambda e: e.tensor_scalar(out=u, in0=kf, scalar1=0.25, scalar2=None, op0=ALU.add), [Bk], [Bu])
                vec(lambda e: e.scalar_tensor_tensor(out=u, in0=u, scalar=1.0, in1=u, op0=ALU.is_ge, op1=ALU.subtract), [Bu], [Bu])

            def table_part2(Ct_sl, St_sl, Bt, tmps, colf, Bc, blend):
                pf, u, kf, Bpf, Bu, Bk = tmps
                act(St_sl, kf, AF.Sin, [Bk, Bc], [Bt], scale=colf(C_SSC), bias=colf(C_SBI))
                act(Ct_sl, u, AF.Sin, [Bu, Beps], [Bt], scale=-SIN_SC, bias=epsc[:, 2:3])
                if blend:
                    vec(lambda e: e.tensor_scalar(out=Ct_sl, in0=Ct_sl, scalar1=colf(C_ROTM), scalar2=colf(C_NROTM),
                                                  op0=ALU.mult, op1=ALU.add), [Bt, Bc], [Bt])

            def table_chunk(pos_sl, Ct_sl, St_sl, Bt, tmps, colf, Bc, blend):
                table_part1(pos_sl, tmps, colf, Bc)
                table_part2(Ct_sl, St_sl, Bt, tmps, colf, Bc, blend)

            def alloc_table_tmp():
                return (A.f32(512), A.f32(512), A.f32(512), Buf(), Buf(), Buf())

            def make_tables(pos_d, T, Ct, St, Bt):
                mark = A.top
                tmps = alloc_table_tmp()
                for c0 in range(0, T, 512):
                    sl = slice(c0, c0 + 512)
                    table_chunk(pos_d[:, sl], Ct[:, sl], St[:, sl], Bt, tmps, col, Bcst, L == 1)
                S.wait_bufs("sync", [Bposi])
                return mark

            rope_tmp = {}

            def rope(pb, Ct_sl, St_sl, Btab, outs, idx):
                rope_tmp["i"] = rope_tmp.get("i", 0) + 1
                tq, t1, t2, Bq, B1, B2 = rope_tmp["t"][rope_tmp["i"] % len(rope_tmp["t"])]
                act(tq, ps[pb], AF.Copy, [Bps[pb]], [Bq])
                rb = nb()
                mm(ps[rb], [(Rm, tq)], [Bmats, Bq], Bps[rb])
                vec(lambda e: e.tensor_tensor(out=t1, in0=ps[pb], in1=Ct_sl, op=ALU.mult), [Bps[pb], Btab], [B1])
                vec(lambda e: e.tensor_tensor(out=t2, in0=ps[rb], in1=St_sl, op=ALU.mult), [Bps[rb], Btab], [B2])
                for i, (o, Bo, psl) in enumerate(outs):
                    eng = "vector" if i == 0 else "gpsimd"
                    vec(lambda e, o=o, psl=psl: e.tensor_tensor(out=o, in0=t1[psl, :], in1=t2[psl, :], op=ALU.add), [B1, B2], [Bo], eng=eng)

            def alloc_rope_tmp(n=2):
                rope_tmp["t"] = [(A.bf16(512), A.f32(512), A.f32(512), Buf(), Buf(), Buf()) for _ in range(n)]

            def rms_tile(pbs, nfeat, gcol0, out_fn, Bout, tmp):
                sq, xf, sd, rstd, Bsq, Bxf, Bsd, Brs = tmp
                nch = len(pbs)
                for i, pb in enumerate(pbs):
                    act(sq[:, i * 512:(i + 1) * 512], ps[pb], AF.Square, [Bps[pb]], [Bsq])
                    vec(lambda e, i=i, pb=pb: e.tensor_copy(out=xf[:, i * 512:(i + 1) * 512], in_=ps[pb]), [Bps[pb]], [Bxf])
                sb = nb()
                mm(ps[sb], [(ones, sq[:, i * 512:(i + 1) * 512]) for i in range(nch)], [Bones, Bsq], Bps[sb])
                act(sd, ps[sb], AF.Ln, [Bps[sb], Beps], [Bsd], scale=1.0 / nfeat, bias=epsc[:, 1:2])
                act(rstd, sd, AF.Exp, [Bsd], [Brs], scale=-0.5)
                for i in range(nch):
                    vec(lambda e, i=i: e.scalar_tensor_tensor(out=out_fn(i), in0=xf[:, i * 512:(i + 1) * 512], scalar=col(gcol0 + i),
                                                              in1=rstd, op0=ALU.mult, op1=ALU.mult), [Bxf, Brs, Bcst], [Bout])

            def alloc_rms_tmp(nch):
                return (A.bf16(nch * 512), A.f32(nch * 512), A.f32(512), A.f32(512), Buf(), Buf(), Buf(), Buf())

            deferred = []

            def flush_deferred():
                for f_ in deferred:
                    f_()
                del deferred[:]

            def attention(m, nmaps, score_pairs, v_fn, Breads, Ptiles, post):
                sbanks = [(0, 1), (2, 3)]
                accs = [(nb((4, 6)) if nmaps == 1 else 4), None]
                if nmaps == 1:
                    ob = [accs[0]]
                    sb_ = [ob[0] + 1]
                else:
                    ob = [4, 6]
                    sb_ = [5, 7]
                ng = 8 * m + 8
                pendq = []

                def pv(items, g, c0, first, last):
                    for mp, (pt, Bpt) in enumerate(items):
                        def f(e, mp=mp, pt=pt):
                            e.matmul(ps[ob[mp]][:, c0:512], lhsT=v_fn(g), rhs=pt[:, c0:512], start=first, stop=last)
                            return e.matmul(ps[sb_[mp]][:, c0:512], lhsT=ones, rhs=pt[:, c0:512], start=first, stop=last)
                        S.op("tensor", f, reads=[Bpt, Bones] + Breads, writes=[Bps[ob[mp]], Bps[sb_[mp]]])

                for g in range(ng):
                    c0 = 0 if g < 8 * m else ((g - 8 * m) // 2) * 128
                    items = []
                    for mp in range(nmaps):
                        bk = sbanks[mp][g % 2] if nmaps == 2 else (0, 1, 2, 3)[g % 4]
                        mm(ps[bk][:, c0:512], score_pairs(mp, g, c0), Breads, Bps[bk])
                        pt, Bpt = Ptiles[(g * nmaps + mp) % len(Ptiles)]
                        act(pt[:, c0:512], ps[bk][:, c0:512], AF.Exp, [Bps[bk]], [Bpt], scale=SCALE)
                        if g >= 8 * m:
                            mk = maskE if (g - 8 * m) % 2 == 0 else maskO
                            vec(lambda e, pt=pt, mk=mk, c0=c0: e.tensor_tensor(out=pt[:, c0:c0 + 128], in0=pt[:, c0:c0 + 128], in1=mk,
                                                                               op=ALU.mult), [Bpt, Bmats], [Bpt], eng="gpsimd")
                        items.append((pt, Bpt))
                    pendq.append((items, g, c0, g == 0, g == ng - 1))
                    if len(pendq) > (1 if nmaps == 2 else 2):
                        pv(*pendq.pop(0))
                    if g == 3:
                        flush_deferred()
                while pendq:
                    pv(*pendq.pop(0))
                post(ob, sb_)

            def layernorm_tile(z, z3, Bzc, gcol, bcol, out_b, Bb, tmp, gbcol=C_LNM_G, bbcol=C_LNM_B):
                ln_stats(z, z3, Bzc, tmp)
                ln_apply(z, z3, Bzc, gcol, bcol, out_b, Bb, tmp, gbcol, bbcol)

            sq2bufs = {}

            def ln_stats(z, z3, Bzc, tmp):
                sq, zb, mean, msq, var, sd, rstd, Bsq, Bzb, Bm, Bv, Bsd, Brs = tmp
                Bsq2 = sq2bufs.setdefault(id(Bsq), Buf("sq_pool"))
                act(v3(sq, 8)[:, 0:5, :], z3[:, 0:5, :], AF.Square, list(Bzc[0:5]), [Bsq])
                for c in range(5, 8):
                    vec(lambda e, c=c: e.tensor_tensor(out=sq[:, c * 512:(c + 1) * 512], in0=z(c), in1=z(c), op=ALU.mult),
                        [Bzc[c]], [Bsq2], eng="gpsimd")
                b1, b2 = nb(), nb()
                mm(ps[b1], [(zb, z(c)) for c in range(8)], [Bzb] + list(Bzc), Bps[b1])
                mm(ps[b2], [(ones, sq[:, c * 512:(c + 1) * 512]) for c in range(8)], [Bones, Bsq, Bsq2], Bps[b2])
                vec(lambda e: e.tensor_scalar(out=mean, in0=ps[b1], scalar1=1.0 / D, scalar2=None, op0=ALU.mult), [Bps[b1]], [Bm])
                vec(lambda e: e.tensor_tensor(out=msq, in0=mean, in1=mean, op=ALU.mult), [Bm], [Bv])
                vec(lambda e: e.scalar_tensor_tensor(out=var, in0=ps[b2], scalar=1.0 / D, in1=msq, op0=ALU.mult, op1=ALU.subtract),
                    [Bps[b2], Bv], [Bv])
                act(sd, var, AF.Ln, [Bv, Beps], [Bsd], bias=epsc[:, 0:1])
                act(rstd, sd, AF.Exp, [Bsd], [Brs], scale=-0.5)

            def ln_apply(z, z3, Bzc, gcol, bcol, out_b, Bb, tmp, gbcol=C_LNM_G, bbcol=C_LNM_B):
                sq, zb, mean, msq, var, sd, rstd, Bsq, Bzb, Bm, Bv, Bsd, Brs = tmp
                for c in range(8):
                    vec(lambda e, c=c: e.tensor_tensor(out=z(c), in0=z(c), in1=mean, op=ALU.subtract), [Bzc[c], Bm], [Bzc[c]],
                        eng=("gpsimd" if c % 4 == 3 else "vector"))
                for c in range(8):
                    vec(lambda e, c=c: e.tensor_tensor(out=z(c), in0=z(c), in1=rstd, op=ALU.mult), [Bzc[c], Brs], [Bzc[c]],
                        eng=("gpsimd" if c % 2 == 0 else "vector"))
                for c in range(8):
                    if out_b is not None:
                        vec(lambda e, c=c: e.tensor_scalar(out=out_b(c), in0=z(c), scalar1=col(gbcol + c), scalar2=col(bbcol + c),
                                                           op0=ALU.mult, op1=ALU.add), [Bzc[c], Bcst], [Bb])
                for c in range(8):
                    act(z(c), z(c), AF.Identity, [Bzc[c], Bcst], [Bzc[c]], scale=col(gcol + c), bias=col(bcol + c))

            def alloc_ln_tmp(o32, Bo32):
                msq_ = A.f32(512)
                return (A.bf16(4096), o32, A.f32(512), msq_, msq_, A.f32(512), A.f32(512),
                        Buf(), Bo32, Buf(), Buf(), Buf(), Buf())


            OT = A.bf16(8 * OWN)
            OT3 = v3(OT, 8)
            BOT = [Buf("OT%d" % m) for m in range(4)]
            base_mark = A.top

            if L == 0:
                SCALE = 192.0 ** -0.5
                ckvT = A.bf16(2 * SEQ)
                ckv3 = v3(ckvT, 2)
                kpe_lo = A.bf16(SEQ)
                kpe_hi = A.bf16(SEQ)
                Bckv, Bklo, Bkhi = Buf("ckv"), Buf("klo"), Buf("khi")
                wdq = v3(A.bf16(8 * 384), 8)
                wuq = v3(A.bf16(3 * 1536), 3)
                Bwdq, Bwuq = Buf("wdq"), Buf("wuq")
                markA = A.top
                vec(lambda e: e.memset(kpe_lo, 0.0), [], [Bklo], eng="gpsimd")
                vec(lambda e: e.memset(kpe_hi, 0.0), [], [Bkhi], eng="gpsimd")
                wdkv = A.bf16(8 * 384)
                wdkv3 = v3(wdkv, 8)
                Bwdkv = Buf("wdkv")
                S.dma("gpsimd", wdkv3, w_dkv.rearrange("(c p) n -> p c n", p=128), Bwdkv)
                Call, Sall = A.f32(SEQ), A.f32(SEQ)
                Btab = Buf("tab_all")
                xa = [(v3(A.bf16(8 * 1024), 8), Buf("xa%d" % i)) for i in range(2)]

                def load_xa(t2):
                    xat, Bxa = xa[t2 % 2]
                    S.dma("gpsimd", xat, xT_all0[:, t2 * 1024:(t2 + 1) * 1024].rearrange("(c p) n -> p c n", p=128), Bxa)

                load_xa(0)
                load_xa(1)
                S.dma("gpsimd", wdq, w_dq.rearrange("(c p) n -> p c n", p=128), Bwdq)
                S.dma("gpsimd", wuq, w_uq.rearrange("(c p) n -> p c n", p=128), Bwuq)
                alloc_rope_tmp(3)
                rtmps = [alloc_rms_tmp(2), alloc_rms_tmp(2)]
                make_tables(pos_all, SEQ, Call, Sall, Btab)
                for t in range(8):
                    ts_ = slice(t * 512, (t + 1) * 512)
                    xat, Bxa = xa[(t // 2) % 2]
                    hs_ = slice((t % 2) * 512, (t % 2) * 512 + 512)
                    pbs = [nb(), nb(), nb()]
                    for oc in range(3):
                        mm(ps[pbs[oc]], [(wdkv3[:, kc, oc * 128:(oc + 1) * 128], xat[:, kc, hs_]) for kc in range(8)], [Bwdkv, Bxa], Bps[pbs[oc]])
                    if t % 2 == 1 and t // 2 + 2 < 4:
                        load_xa(t // 2 + 2)
                    rms_tile(pbs[0:2], 256.0, C_X0 + 3, lambda i, ts_=ts_: ckv3[:, i, ts_], Bckv, rtmps[t % 2])
                    rope(pbs[2], Call[:, ts_], Sall[:, ts_], Btab,
                         [(kpe_lo[0:64, ts_], Bklo, slice(0, 64)), (kpe_hi[64:128, ts_], Bkhi, slice(64, 128))], t)
                S.barrier()
                A.top = markA
                qn = A.bf16(8 * OWN)
                qn3 = v3(qn, 8)
                qr = A.bf16(4 * OWN)
                qr3 = v3(qr, 4)
                Bqn, Bqr = Buf("qn"), Buf("qr")
                markB = A.top
                Cown, Sown = A.f32(OWN), A.f32(OWN)
                Btabo = Buf("tab_own")
                tm = make_tables(pos_own, OWN, Cown, Sown, Btabo)
                A.top = tm
                S.barrier()
                cqT = v3(A.bf16(3 * 512), 3)
                Bcq = Buf("cq")
                xo = [(v3(A.bf16(8 * 1024), 8), Buf("xo%d" % i)) for i in range(2)]
                for t2 in range(2):
                    S.dma("gpsimd", xo[t2][0], xT_own[:, t2 * 1024:(t2 + 1) * 1024].rearrange("(c p) n -> p c n", p=128), xo[t2][1])
                alloc_rope_tmp(2)
                rtmps = [alloc_rms_tmp(3)] * 2
                for m in range(4):
                    ms = slice(m * 512, (m + 1) * 512)
                    xot, Bxo = xo[m // 2]
                    hs_ = slice((m % 2) * 512, (m % 2) * 512 + 512)
                    pbs = [nb(), nb(), nb()]
                    for oc in range(3):
                        mm(ps[pbs[oc]], [(wdq[:, kc, oc * 128:(oc + 1) * 128], xot[:, kc, hs_]) for kc in range(8)], [Bwdq, Bxo], Bps[pbs[oc]])
                    rms_tile(pbs, 384.0, C_X0, lambda i: cqT[:, i, :], Bcq, rtmps[m % 2])
                    for hh in range(8):
                        pb = nb()
                        mm(ps[pb], [(wuq[:, kc, hh * 128:(hh + 1) * 128], cqT[:, kc, :]) for kc in range(3)], [Bwuq, Bcq], Bps[pb])
                        evac(qn3[:, hh, ms], ps[pb], [Bps[pb]], [Bqn])
                    for pr in range(4):
                        pb = nb()
                        mm(ps[pb], [(wuq[:, kc, 1024 + pr * 128:1024 + (pr + 1) * 128], cqT[:, kc, :]) for kc in range(3)], [Bwuq, Bcq], Bps[pb])
                        rope(pb, Cown[:, ms], Sown[:, ms], Btabo, [(qr3[:, pr, ms], Bqr, slice(0, 128))], pr)
                S.barrier()
                A.top = markB
                wo_pre_off = A.top
                wo_pre = v3(A.bf16(8 * D), 8)
                wo_pre_end = A.top
                Bwo_pre = Buf("wo_pre")
                wukv = v3(A.bf16(2 * 2048), 2)
                Bwukv = Buf("wukv")
                S.dma("gpsimd", wukv, w_ukv.rearrange("(c p) n -> p c n", p=128), Bwukv)
                for c in range(8):
                    S.dma("gpsimd", wo_pre[:, c, :], w_o[c * 128:(c + 1) * 128, :], Bwo_pre)
                KT = A.bf16(SEQ)
                Vh = A.bf16(32 * 128)
                BKT, BV = Buf("KT"), Buf("V")
                Ptiles = [(A.bf16(512), Buf("P%d" % i)) for i in range(4)]
                rs_t = [(A.f32(512), Buf("rs%d" % i)) for i in range(2)]
                cst1s = A.f32(NCST)
                Bcst1 = Buf("cst1s")
                S.dma("sync", cst1s, cst_ds[1], Bcst1)
                ttmps = alloc_table_tmp()
                tstage = [(A.bf16(512), A.bf16(512), Buf("tst%d" % i)) for i in range(2)]
                tjobs = [(pos_alls[1], c0, c0) for c0 in range(0, SEQ, 512)] + [(pos_own, c0, SEQ + c0) for c0 in range(0, OWN, 512)]

                c1col = lambda i: cst1s[:, i:i + 1]

                def early_table_step(i):
                    if 1 <= i <= len(tjobs):
                        pd, c0, o0 = tjobs[i - 1]
                        Cst, Sst, Bst = tstage[(i - 1) % 2]
                        table_part2(Cst, Sst, Bst, ttmps, c1col, Bcst1, True)
                        S.dma("sync", tabC_d[:, o0:o0 + 512], Cst, Btabd, src=Bst)
                        S.dma("sync", tabS_d[:, o0:o0 + 512], Sst, Btabd, src=Bst)
                    if i < len(tjobs):
                        pd, c0, o0 = tjobs[i]
                        table_part1(pd[:, c0:c0 + 512], ttmps, c1col, Bcst1)

                for hh in range(8):
                    for t in range(8):
                        ts_ = slice(t * 512, (t + 1) * 512)
                        pb = nb()
                        mm(ps[pb], [(wukv[:, kc, hh * 128:(hh + 1) * 128], ckv3[:, kc, ts_]) for kc in range(2)], [Bwukv, Bckv], Bps[pb])
                        evac(KT[:, ts_], ps[pb], [Bps[pb]], [BKT])
                    for gb in range(8):
                        pb = nb()

                        def fv(e, gb=gb, pb=pb, hh=hh):
                            ins = None
                            for j in range(4):
                                g = gb * 4 + j
                                for kc in range(2):
                                    ins = e.matmul(ps[pb][:, j * 128:(j + 1) * 128], lhsT=ckv3[:, kc, g * 128:(g + 1) * 128],
                                                   rhs=wukv[:, kc, 1024 + hh * 128:1024 + (hh + 1) * 128], start=(kc == 0), stop=(kc == 1))
                            return ins
                        S.op("tensor", fv, reads=[Bwukv, Bckv], writes=[Bps[pb]])
                        evac(Vh[:, gb * 512:(gb + 1) * 512], ps[pb], [Bps[pb]], [BV])
                    kpe = kpe_lo if hh % 2 == 0 else kpe_hi
                    Bkpe = Bklo if hh % 2 == 0 else Bkhi
                    for m in range(4):
                        early_table_step(hh * 4 + m)

                        def sp(mp, g, c0, hh=hh, m=m, kpe=kpe):
                            gs = slice(g * 128, (g + 1) * 128)
                            return [(KT[:, gs], qn3[:, hh, m * 512 + c0:(m + 1) * 512]),
                                    (kpe[:, gs], qr3[:, hh // 2, m * 512 + c0:(m + 1) * 512])]

                        def post(ob, sb_, hh=hh, m=m):
                            rs, Brs = rs_t[m % 2]
                            act(rs, ps[sb_[0]], AF.Ln, [Bps[sb_[0]]], [Brs])
                            act(rs, rs, AF.Exp, [Brs], [Brs], scale=-1.0)
                            vec(lambda e: e.tensor_tensor(out=OT3[:, hh, m * 512:(m + 1) * 512], in0=ps[ob[0]], in1=rs, op=ALU.mult),
                                [Bps[ob[0]], Brs], [BOT[m]])
                        attention(m, 1, sp, lambda g: Vh[:, g * 128:(g + 1) * 128], [BKT, BV, Bkpe, Bqn, Bqr], Ptiles, post)
            else:
                SCALE = 64.0 ** -0.5
                xall = A.bf16(8 * SEQ)
                xall3 = v3(xall, 8)
                Bxall = Buf("xall")
                Cb, Sb = A.bf16(SEQ), A.bf16(SEQ)
                Btab = Buf("tab_all")
                qall_off = A.top
                qall = A.bf16(8 * OWN)
                qall3 = v3(qall, 8)
                Bqall2 = [Buf("qall_lo"), Buf("qall_hi")]
                wo1 = v3(qall[:, 0:8 * D], 8)
                markB = A.top
                wq, Bwq = wq_top, Bwq_top
                Cown, Sown = A.bf16(OWN), A.bf16(OWN)
                Btabo = Buf("tab_own")
                S.dma("sync", Cown, tabC_d[:, SEQ:SEQ + OWN], Btabo, src=Btabd)
                S.dma("sync", Sown, tabS_d[:, SEQ:SEQ + OWN], Btabo, src=Btabd)
                S.dma("sync", Cb, tabC_d[:, 0:SEQ], Btab, src=Btabd)
                S.dma("sync", Sb, tabS_d[:, 0:SEQ], Btab, src=Btabd)
                xo = [(v3(A.bf16(8 * 512), 8), Buf("xo%d" % i)) for i in range(2)]
                alloc_rope_tmp(3)
                assert A.top <= WQ_OFF, (A.top, WQ_OFF)
                for m in range(4):
                    ms = slice(m * 512, (m + 1) * 512)
                    xot, Bxo = xo[m % 2]
                    for hf in range(2):
                        S.dma("sync", xot[:, 4 * hf:4 * hf + 4, :], xb_own_t[m][hf].ap().rearrange("(c p) n -> p c n", p=128),
                              Bxo, src=Bxbown[m][hf])
                    for hh in range(8):
                        pb = nb()
                        mm(ps[pb], [(wq[:, kc, hh * 128:(hh + 1) * 128], xot[:, kc, :]) for kc in range(8)], [Bwq, Bxo], Bps[pb])
                        rope(pb, Cown[:, ms], Sown[:, ms], Btabo, [(qall3[:, hh, ms], Bqall2[hh // 4], slice(0, 128))], hh)
                for m in range(4):
                    for hf in range(2):
                        for r_ in range(2):
                            S.dma("sync", xall3[:, 4 * hf:4 * hf + 4, r_ * 2048 + m * 512:r_ * 2048 + (m + 1) * 512],
                                  xb_pair_t[m][hf].ap()[r_ * 512:(r_ + 1) * 512, :].rearrange("(c p) n -> p c n", p=128),
                                  Bxall, src=Bxbpair[m][hf])
                S.barrier()
                A.top = markB
                wkv = [(v3(A.bf16(8 * 128), 8), v3(A.bf16(8 * 128), 8), Buf("wk%d" % i), Buf("wv%d" % i)) for i in range(2)]
                KT = A.bf16(SEQ)
                Vh = A.bf16(32 * 128)
                BKT, BV = Buf("KT"), Buf("V")
                q1p, q2p = A.bf16(OWN), A.bf16(OWN)
                Bq1, Bq2 = Buf("q1p"), Buf("q2p")
                vec(lambda e: e.memset(q1p, 0.0), [], [Bq1], eng="gpsimd")
                vec(lambda e: e.memset(q2p, 0.0), [], [Bq2], eng="gpsimd")
                Ptiles = [(A.bf16(512), Buf("P%d" % i)) for i in range(4)]
                alloc_rope_tmp(2)
                ftmp = [(A.f32(512), Buf("f%d" % i)) for i in range(4)]
                sqo = A.bf16(512)
                Bsqo = Buf("sqo")
                def load_kv(hh):
                    wk, wv, Bwk, Bwv = wkv[hh % 2]
                    S.dma("gpsimd", wk, w_qkv[:, 1024 + hh * 128:1024 + (hh + 1) * 128].rearrange("(c p) n -> p c n", p=128), Bwk)
                    S.dma("gpsimd", wv, w_qkv[:, 2048 + hh * 128:2048 + (hh + 1) * 128].rearrange("(c p) n -> p c n", p=128), Bwv)

                load_kv(0)
                for hh in range(8):
                    wk, wv, Bwk, Bwv = wkv[hh % 2]
                    for t in range(8):
                        ts_ = slice(t * 512, (t + 1) * 512)
                        pb = nb((0, 1, 2, 3))
                        mm(ps[pb], [(wk[:, kc, :], xall3[:, kc, ts_]) for kc in range(8)], [Bwk, Bxall], Bps[pb])
                        rope(pb, Cb[:, ts_], Sb[:, ts_], Btab, [(KT[:, ts_], BKT, slice(0, 128))], t)
                        gb = t
                        pb = nb((4, 5, 6, 7))

                        def fv(e, gb=gb, pb=pb, wv=wv):
                            ins = None
                            for j in range(4):
                                g = gb * 4 + j
                                for kc in range(8):
                                    ins = e.matmul(ps[pb][:, j * 128:(j + 1) * 128], lhsT=xall3[:, kc, g * 128:(g + 1) * 128],
                                                   rhs=wv[:, kc, :], start=(kc == 0), stop=(kc == 7))
                            return ins
                        S.op("tensor", fv, reads=[Bwv, Bxall], writes=[Bps[pb]])
                        evac(Vh[:, gb * 512:(gb + 1) * 512], ps[pb], [Bps[pb]], [BV])
                    if hh + 1 < 8:
                        load_kv(hh + 1)
                    vec(lambda e, hh=hh: e.tensor_copy(out=q1p[0:64, :], in_=qall3[0:64, hh, :]), [Bqall2[hh // 4]], [Bq1])
                    vec(lambda e, hh=hh: e.tensor_copy(out=q2p[64:128, :], in_=qall3[64:128, hh, :]), [Bqall2[hh // 4]], [Bq2], eng="gpsimd")
                    if hh == 4:
                        for c in range(8):
                            S.dma("gpsimd", wo1[:, c, :], w_o[c * 128:(c + 1) * 128, :], Bqall2[0])
                    for m in range(4):
                        def sp(mp, g, c0, m=m):
                            gs = slice(kmap(g) * 128, (kmap(g) + 1) * 128)
                            qp = q1p if mp == 0 else q2p
                            return [(KT[:, gs], qp[:, m * 512 + c0:(m + 1) * 512])]

                        def post(ob, sb_, hh=hh, m=m):
                            (r1, Br1), (r2, Br2), (o1, Bo1), (o2, Bo2) = ftmp
                            sd, Bsd = r2, Br2
                            act(r1, ps[sb_[0]], AF.Ln, [Bps[sb_[0]]], [Br1])
                            act(r1, r1, AF.Exp, [Br1], [Br1], scale=-1.0)
                            act(r2, ps[sb_[1]], AF.Ln, [Bps[sb_[1]]], [Br2])
                            act(r2, r2, AF.Exp, [Br2], [Br2], scale=-1.0)
                            vec(lambda e: e.tensor_tensor(out=o1, in0=ps[ob[0]], in1=r1, op=ALU.mult), [Bps[ob[0]], Br1], [Bo1])
                            vec(lambda e: e.tensor_tensor(out=o2, in0=ps[ob[1]], in1=r2, op=ALU.mult), [Bps[ob[1]], Br2], [Bo2])
                            vec(lambda e: e.scalar_tensor_tensor(out=o1, in0=o2, scalar=col(C_T0 + 1), in1=o1, op0=ALU.mult, op1=ALU.add),
                                [Bo1, Bo2, Bcst], [Bo1])

                            def tail(hh=hh, m=m):
                                act(sqo, o1, AF.Square, [Bo1], [Bsqo])
                                pb = nb((0, 1, 2, 3))
                                mm(ps[pb], [(ones, sqo)], [Bones, Bsqo], Bps[pb])
                                act(sd, ps[pb], AF.Ln, [Bps[pb], Beps], [Bsd], scale=1.0 / 128.0, bias=epsc[:, 1:2])
                                act(r1, sd, AF.Exp, [Bsd], [Br1], scale=-0.5)
                                vec(lambda e: e.scalar_tensor_tensor(out=OT3[:, hh, m * 512:(m + 1) * 512], in0=o1, scalar=col(C_T0), in1=r1,
                                                                     op0=ALU.mult, op1=ALU.mult), [Bo1, Br1, Bcst], [BOT[m]])
                            deferred.append(tail)
                        attention(m, 2, sp, lambda g: Vh[:, kmap(g) * 128:(kmap(g) + 1) * 128], [BKT, BV, Bq1, Bq2], Ptiles, post)

            flush_deferred()
            S.barrier()
            A.top = base_mark
            resid = A.f32(8 * OWN)
            resid3 = v3(resid, 8)
            Bres = [[Buf("res%d_%d" % (m, c)) for c in range(8)] for m in range(4)]
            slot0_off = A.top
            wu0 = v3(A.bf16(8 * 1024), 8)
            Bwu0 = Buf("wu0")
            markD = A.top
            if L == 0:
                assert A.top <= wo_pre_off, (A.top, wo_pre_off)
                A.top = wo_pre_end
                wo, Bwo = wo_pre, Bwo_pre
            else:
                assert A.top == qall_off, (A.top, qall_off)
                A.top += 4096
                wo, Bwo = wo1, Bqall2[0]
            Bxload = [Buf("xload%d" % i) for i in range(3)]

            def xload(i_):
                a_, b_ = [(0, 512), (512, 1024), (1024, 2048)][i_]
                S.dma("sync", resid3[:, :, a_:b_], xT_own[:, a_:b_].rearrange("(c p) n -> p c n", p=128), Bxload[i_], src=Bxres_src)

            xload(0)
            o32 = A.f32(128)
            Bo32 = Buf("ones32")
            vec(lambda e: e.memset(o32, 1.0), [], [Bo32])
            lnts = [alloc_ln_tmp(o32, Bo32), alloc_ln_tmp(o32, Bo32)]
            def zargs(m):
                ms = slice(m * 512, (m + 1) * 512)
                return (lambda c, ms=ms: resid3[:, c, ms]), resid3[:, :, ms], Bres[m]

            pend_ln = None
            for m in range(4):
                ms = slice(m * 512, (m + 1) * 512)
                if m == 1:
                    xload(1)
                    xload(2)
                    for c in range(8):
                        S.dma("gpsimd", wu0[:, c, :], w_up[c * 128:(c + 1) * 128, 0:1024], Bwu0)
                for dc in range(8):
                    pb = nb()
                    mm(ps[pb], [(wo[:, hh, dc * 128:(dc + 1) * 128], OT3[:, hh, ms]) for hh in range(8)], [Bwo, BOT[m]], Bps[pb])
                    vec(lambda e, dc=dc, pb=pb, ms=ms: e.scalar_tensor_tensor(out=resid3[:, dc, ms], in0=resid3[:, dc, ms], scalar=ALPHA,
                                                                              in1=ps[pb], op0=ALU.mult, op1=ALU.add),
                        [Bxload[min(m, 2)], Bps[pb]], [Bres[m][dc]])
                z, z3, Bzc = zargs(m)
                ln_stats(z, z3, Bzc, lnts[m % 2])
                if pend_ln is not None:
                    pm = pend_ln
                    z, z3, Bzc = zargs(pm)
                    ln_apply(z, z3, Bzc, C_GA, C_BA, lambda c, pm=pm: OT3[:, c, pm * 512:(pm + 1) * 512], BOT[pm], lnts[pm % 2])
                pend_ln = m
            pm = pend_ln
            z, z3, Bzc = zargs(pm)
            ln_apply(z, z3, Bzc, C_GA, C_BA, lambda c, pm=pm: OT3[:, c, pm * 512:(pm + 1) * 512], BOT[pm], lnts[pm % 2])
            S.barrier()
            A.top = markD
            wslots = [(wu0, v3(A.bf16(8 * 1024), 8), Bwu0, Buf("wd0")),
                      (v3(A.bf16(8 * 1024), 8), v3(A.bf16(8 * 1024), 8), Buf("wu1"), Buf("wd1"))]
            hTs = [(v3(A.bf16(8 * 512), 8), Buf("hT%d" % i)) for i in range(2)]
            rts = [(A.f32(512), Buf("r%d" % i)) for i in range(3)]
            rr = [0]

            def up_proj(fq, m, slot):
                wu, wd, Bwu, Bwd = wslots[slot]
                hT, BhT = hTs[m % 2]
                ms = slice(m * 512, (m + 1) * 512)
                for fc in range(8):
                    pb = nb()
                    mm(ps[pb], [(wu[:, kc, fc * 128:(fc + 1) * 128], OT3[:, kc, ms]) for kc in range(8)], [Bwu, BOT[m]], Bps[pb])
                    rr[0] += 1
                    r, Br = rts[rr[0] % 3]
                    act(r, ps[pb], AF.Relu, [Bps[pb]], [Br])
                    vec(lambda e, r=r, fc=fc, hT=hT: e.tensor_tensor(out=hT[:, fc, :], in0=r, in1=r, op=ALU.mult), [Br], [BhT])

            def down_proj(fq, m, slot):
                wu, wd, Bwu, Bwd = wslots[slot]
                hT, BhT = hTs[m % 2]
                ms = slice(m * 512, (m + 1) * 512)
                for dc in range(8):
                    pb = nb()
                    mm(ps[pb], [(wd[:, fc, dc * 128:(dc + 1) * 128], hT[:, fc, :]) for fc in range(8)], [Bwd, BhT], Bps[pb])
                    vec(lambda e, dc=dc, pb=pb, ms=ms: e.tensor_tensor(out=resid3[:, dc, ms], in0=resid3[:, dc, ms], in1=ps[pb], op=ALU.add),
                        [Bps[pb], Bres[m][dc]], [Bres[m][dc]])

            def load_w(fq):
                wu, wd, Bwu, Bwd = wslots[fq % 2]
                for c in range(8 if fq > 0 else 0):
                    S.dma("gpsimd", wu[:, c, :], w_up[c * 128:(c + 1) * 128, fq * 1024:(fq + 1) * 1024], Bwu)
                for c in range(8):
                    S.dma("gpsimd", wd[:, c, :], w_down[fq * 1024 + c * 128:fq * 1024 + (c + 1) * 128, :], Bwd)

            def _f(o, n):
                return arena_t[:, slot0_off + o:slot0_off + o + n]
            sq_ = _f(0, 2048).bitcast(BF16)
            zb_ = _f(2048, 128)
            Bsq_, Bzb_ = Buf("sq2"), Buf("ones32b")
            lnts = []
            for i_ in range(2):
                o_ = 4096 + i_ * 2048
                msq_ = _f(o_ + 512, 512)
                lnts.append((sq_, zb_, _f(o_, 512), msq_, msq_, _f(o_ + 1024, 512), _f(o_ + 1536, 512),
                             Bsq_, Bzb_, Buf(), Buf(), Buf(), Buf()))

            def ln2_fence():
                allb = [Bsq_, Bzb_] + [b_ for t_ in lnts for b_ in t_[9:]]
                act(sq_[:, 0:8], sq_[:, 8:16], AF.Copy, [], [wslots[0][2], wslots[0][3]] + allb)
                vec(lambda e: e.memset(zb_, 1.0), [], [Bzb_])
            By = Buf("y")
            pend_cc = None

            def emit_cc(m_):
                for hf in range(2):
                    S.collective(lambda e, m_=m_, hf=hf: e.collective_compute("AllGather", ALU.bypass,
                                                                               replica_groups=[[0, 1], [2, 3], [4, 5], [6, 7]],
                                                                               ins=[xb_own_t[m_][hf].ap().opt()],
                                                                               outs=[xb_pair_t[m_][hf].ap().opt()]),
                                 [Bxbown[m_][hf]], Bxbpair[m_][hf])

            def finish_tile(m):
                nonlocal pend_cc
                ms = slice(m * 512, (m + 1) * 512)
                z, z3, Bzc = zargs(m)
                if L == 0:
                    ln_apply(z, z3, Bzc, C_LNF_G, C_LNF_B, lambda c, ms=ms: OT3[:, c, ms], BOT[m], lnts[m % 2], C_LNF_G, C_LNF_B)
                    S.dma("sync", x1_own[:, ms].rearrange("(c p) n -> p c n", p=128), resid3[:, :, ms], Bx1own, src=Bres[m])
                    for hf in range(2):
                        S.dma("sync", xb_own_t[m][hf].ap().rearrange("(c p) n -> p c n", p=128), OT3[:, 4 * hf:4 * hf + 4, ms],
                              Bxbown[m][hf], src=BOT[m])
                    if pend_cc is not None:
                        emit_cc(pend_cc)
                    pend_cc = m
                else:
                    ln_apply(z, z3, Bzc, C_LNF_G, C_LNF_B, None, None, lnts[m % 2])
                    S.dma("sync", yT[:, ms].rearrange("(c p) n -> p c n", p=128), resid3[:, :, ms], By, src=Bres[m])


            def ln2_step(m):
                z, z3, Bzc = zargs(m)
                ln_stats(z, z3, Bzc, lnts[m % 2])
                if m > 0:
                    finish_tile(m - 1)

            pend = None
            load_w(0)
            for fq in range(4):
                slot = fq % 2
                for m in range(4):
                    up_proj(fq, m, slot)
                    if pend is not None:
                        down_proj(*pend)
                        if pend[0] == 3:
                            ln2_step(pend[1])
                    pend = (fq, m, slot)
                    if m == 0 and fq + 1 < 4:
                        load_w(fq + 1)
                    if L == 0 and fq == 2 and m == 1:
                        assert A.top <= WQ_OFF, (A.top, WQ_OFF)
                        S.dma("gpsimd", wq_top, w_qkv[:, 0:1024].rearrange("(c p) n -> p c n", p=128), Bwq_top)
                    if fq == 3 and m == 0:
                        ln2_fence()
            down_proj(*pend)
            ln2_step(3)
            finish_tile(3)
            if L == 0:
                emit_cc(pend_cc)
                return
            S.wait_bufs("sync", [By])
            S.barrier()

        emit_layer(0)
        S.barrier(exclude=[b_ for row in Bxbpair for b_ in row])
        emit_layer(1)
        S.emit()
    return nc


def _own_idx(h):
    return np.concatenate([np.arange(128) + 128 * (2 * j + h) for j in range(16)])


def _cols(v):
    return np.ascontiguousarray(v.reshape(-1, 128).T)


def _consts(L, inp):
    cst = np.zeros((128, NCST), np.float32)
    p = np.arange(128)
    if L == 0:
        i = p % 32
        invf = (np.float32(10000.0) ** (-(2 * i).astype(np.float32) / np.float32(64))).astype(np.float32)
        sign = np.where(p % 64 < 32, -1.0, 1.0)
        rot = np.ones(128)
        partner = np.where(p % 64 < 32, p + 32, p - 32)
    else:
        i = p % 8
        invf = (np.float32(500000.0) ** (-(2 * i).astype(np.float32) / np.float32(16))).astype(np.float32)
        rot = (p % 64 < 16).astype(np.float64)
        invf = invf * rot
        sign = np.where(p % 64 < 8, -1.0, 1.0) * rot
        partner = np.where(p % 64 < 8, p + 8, p - 8)
    cst[:, C_INVF] = invf
    cst[:, C_SSC] = sign * SIN_SC
    cst[:, C_SBI] = sign * SIN_BI
    cst[:, C_ROTM] = rot
    cst[:, C_NROTM] = 1.0 - rot
    cst[:, C_LNM_G:C_LNM_G + 8] = _cols(inp["ln_mix_g"][L])
    cst[:, C_LNM_B:C_LNM_B + 8] = _cols(inp["ln_mix_b"][L])
    cst[:, C_LNF_G:C_LNF_G + 8] = _cols(inp["ln_ffn_g"][L])
    cst[:, C_LNF_B:C_LNF_B + 8] = _cols(inp["ln_ffn_b"][L])
    if L == 0:
        cst[:, C_X0:C_X0 + 3] = _cols(inp["mla_q_norm"][0])
        cst[:, C_X0 + 3:C_X0 + 5] = _cols(inp["mla_kv_norm"][0])
    else:
        cst[:, C_X0] = inp["diff_subln"][0]
    R = np.zeros((128, 128), np.float32)
    for m_ in range(128):
        if rot[m_] > 0:
            R[partner[m_], m_] = 1.0
    return cst, R


def _masks(h):
    k = np.arange(128)[:, None]
    q = np.arange(128)[None, :]
    tri = (k <= q).astype(np.float32)
    if h == 0:
        return tri, np.zeros((128, 128), np.float32)
    return np.ones((128, 128), np.float32), tri


def _layer_weights(L, inp):
    w = {}
    if L == 0:
        w["w_dq"] = np.ascontiguousarray(inp["mla_w_dq"][0])
        uq = inp["mla_w_uq"][0].reshape(384, 8, 192)
        w["w_uq"] = np.ascontiguousarray(np.concatenate([uq[:, :, :128].reshape(384, 1024), uq[:, :, 128:].reshape(384, 512)], axis=1))
        dkv = inp["mla_w_dkv"][0]
        w["w_dkv"] = np.ascontiguousarray(np.concatenate([dkv, dkv[:, 256:320]], axis=1))
        ukv = inp["mla_w_ukv"][0].reshape(256, 8, 256)
        w["w_ukv"] = np.ascontiguousarray(np.concatenate([ukv[:, :, :128].reshape(256, 1024), ukv[:, :, 128:].reshape(256, 1024)], axis=1))
        w["w_o"] = np.ascontiguousarray(inp["mla_w_o"][0])
    else:
        w["w_qkv"] = np.ascontiguousarray(inp["diff_w_qkv"][0])
        w["lam"] = np.ascontiguousarray(np.concatenate([inp["diff_lambda_q1"][0], inp["diff_lambda_k1"][0],
                                                         inp["diff_lambda_q2"][0], inp["diff_lambda_k2"][0]]).reshape(1, 256))
        w["w_o"] = np.ascontiguousarray(inp["diff_w_o"][0])
    w["w_up"] = np.ascontiguousarray(inp["ffn_w_up"][L])
    w["w_down"] = np.ascontiguousarray(inp["ffn_w_down"][L])
    return w


_NC_CACHE = {}


def kernel(**inputs):
    inp = {k: np.asarray(v) for k, v in inputs.items()}
    x = inp["x"].astype(np.float32, copy=False)
    positions = inp["positions"]
    if "nc" not in _NC_CACHE:
        _NC_CACHE["nc"] = build_fused()
    nc = _NC_CACHE["nc"]
    cst0, R0 = _consts(0, inp)
    cst1, R1 = _consts(1, inp)
    w0 = _layer_weights(0, inp)
    w1 = _layer_weights(1, inp)
    shared = {
        "cst0": cst0, "cst1": cst1,
        "w_dq": w0["w_dq"], "w_uq": w0["w_uq"], "w_dkv": w0["w_dkv"], "w_ukv": w0["w_ukv"],
        "w_qkv": w1["w_qkv"], "lam": w1["lam"],
        "w_o0": w0["w_o"], "w_o1": w1["w_o"], "w_up0": w0["w_up"], "w_up1": w1["w_up"],
        "w_down0": w0["w_down"], "w_down1": w1["w_down"],
    }
    perm = np.concatenate([_own_idx(0), _own_idx(1)])
    in_maps = []
    for c in range(8):
        b, h = c // 2, c % 2
        own = _own_idx(h)
        xT = np.ascontiguousarray(x[b].T)
        mE, mO = _masks(h)
        pos = positions[b].astype(np.int32)
        d = {
            "xT_all": xT,
            "xT_own": np.ascontiguousarray(xT[:, own]),
            "pos_all0": np.ascontiguousarray(pos.reshape(1, SEQ)),
            "pos_all1": np.ascontiguousarray(pos[perm].reshape(1, SEQ)),
            "pos_own": np.ascontiguousarray(pos[own].reshape(1, OWN)),
            "mats0": np.ascontiguousarray(np.concatenate([mE, mO, R0], axis=1)),
            "mats1": np.ascontiguousarray(np.concatenate([mE, mO, R1], axis=1)),
        }
        d.update(shared)
        in_maps.append(d)
    res = run_bass_kernel_spmd(nc, in_maps, core_ids=list(range(8)))
    out = np.empty((4, SEQ, D), np.float32)
    for c in range(8):
        b, h = c // 2, c % 2
        out[b, _own_idx(h), :] = res.results[c]["yT"].T
    return out
```
